# Optimizing a Trainium2 kernel written in Bass

```python
import jax, jax.numpy as jnp
from jax import lax
import numpy as np

D_MODEL = 1024
BATCH = 8
SEQ = 2048
DEPTH = 1
DEC_BATCH = 128
DEC_SEQ = 8
PAST_LEN = 16384
PAGE_SIZE = 128

D_MIX = D_MODEL
D_POOL = D_MIX // 2
D_CONV = D_MIX - D_POOL
POOL_WINDOWS = (2, 4, 8, 16)
N_POOL_GROUPS = len(POOL_WINDOWS)
POOL_GROUP = D_POOL // N_POOL_GROUPS
POOL_HIST = max(POOL_WINDOWS) - 1
CONV_WIDTH = 3
CONV_HIST = CONV_WIDTH - 1
N_CONV_HEADS = 4
D_FF = ((8 * D_MODEL // 3 + 127) // 128) * 128
N_MEM = 256
N_XHEADS = 4
XHEAD_DIM = D_MODEL // N_XHEADS
EPS = 1e-6

kernel_name = "hybrid_pool_conv_macaron_xattn_step"


def _rms(x, g):
    xf = x.astype(jnp.float32)
    y = xf * lax.rsqrt(jnp.mean(xf * xf, axis=-1, keepdims=True) + EPS)
    return (y * g.astype(jnp.float32)).astype(x.dtype)


def _swiglu_half(x, g, w_in, w_out):
    h = _rms(x, g)
    gate, up = jnp.split(h @ w_in, 2, axis=-1)
    return x + 0.5 * ((jax.nn.silu(gate) * up) @ w_out)


def _pool_mix(a, hist, pos0, pool_w, pool_scale):
    b, t, _ = a.shape
    ext = jnp.concatenate([hist, a], axis=1)
    cs = jnp.cumsum(ext.astype(jnp.float32), axis=1)
    cs = jnp.concatenate([jnp.zeros((b, 1, D_POOL), jnp.float32), cs], axis=1)
    end = cs[:, POOL_HIST + 1:]
    pos = pos0 + jnp.arange(t, dtype=jnp.int32)
    af = a.astype(jnp.float32)
    outs = []
    for gi, w in enumerate(POOL_WINDOWS):
        sl = slice(gi * POOL_GROUP, (gi + 1) * POOL_GROUP)
        start = cs[:, POOL_HIST + 1 - w: POOL_HIST + 1 - w + t, sl]
        cnt = jnp.minimum(pos + 1, w).astype(jnp.float32)[None, :, None]
        mean = (end[..., sl] - start) / cnt
        outs.append(mean - af[..., sl])
    d = jnp.stack(outs, axis=2).astype(a.dtype)
    y = jnp.einsum('btgc,gcd->btgd', d, pool_w).reshape(b, t, D_POOL)
    return y * pool_scale, ext[:, -POOL_HIST:]


def _short_conv(u, hist, conv_w):
    t = u.shape[1]
    ext = jnp.concatenate([hist, u], axis=1)
    y = sum(conv_w[k] * ext[:, k:k + t] for k in range(CONV_WIDTH))
    return y, ext[:, -CONV_HIST:]


def _cross_attn(x, g_xq, w_xq, w_xo, mem_k, mem_v):
    b, t, _ = x.shape
    q = (_rms(x, g_xq) @ w_xq).reshape(b, t, N_XHEADS, XHEAD_DIM)
    s = jnp.einsum('bthd,bmhd->bhtm', q.astype(jnp.float32), mem_k.astype(jnp.float32)) * (XHEAD_DIM ** -0.5)
    p = jax.nn.softmax(s, axis=-1).astype(x.dtype)
    o = jnp.einsum('bhtm,bmhd->bthd', p, mem_v).reshape(b, t, N_XHEADS * XHEAD_DIM)
    return x + o @ w_xo


def _layer(x, pos0, pool_hist, conv_hist, mem_k, mem_v, lw):
    (g_ffn1, w_ffn1_in, w_ffn1_out, g_mix, w_mix_in, pool_w, pool_scale, conv_w,
     w_mix_out, g_xq, w_xq, w_xo, g_ffn2, w_ffn2_in, w_ffn2_out) = lw
    x = _swiglu_half(x, g_ffn1, w_ffn1_in, w_ffn1_out)
    z = _rms(x, g_mix) @ w_mix_in
    a = z[..., :D_POOL]
    cb, cc, ch = jnp.split(z[..., D_POOL:], 3, axis=-1)
    p, new_pool = _pool_mix(a, pool_hist, pos0, pool_w, pool_scale)
    yc, new_conv = _short_conv(cc * ch, conv_hist, conv_w)
    x = x + jnp.concatenate([p, cb * yc], axis=-1) @ w_mix_out
    x = _cross_attn(x, g_xq, w_xq, w_xo, mem_k, mem_v)
    x = _swiglu_half(x, g_ffn2, w_ffn2_in, w_ffn2_out)
    return x, new_pool, new_conv


def setup_inputs(seed: int = 0) -> dict:
    key = jax.random.key(seed)
    ks = iter(jax.random.split(key, 40))
    f32 = jnp.float32

    def nrm(shape, fan_in):
        return jax.random.normal(next(ks), shape, f32) * (fan_in ** -0.5)

    def gain(shape):
        return 1.0 + 0.05 * jax.random.normal(next(ks), shape, f32)

    L = DEPTH
    return {
        "x_prompt": jax.random.normal(next(ks), (BATCH, SEQ, D_MODEL), f32),
        "x_sample": jax.random.normal(next(ks), (DEC_BATCH, DEC_SEQ, D_MODEL), f32),
        "mem_prompt": jax.random.normal(next(ks), (BATCH, N_MEM, D_MODEL), f32),
        "state_pool": jax.random.normal(next(ks), (L, DEC_BATCH, POOL_HIST, D_POOL), f32),
        "state_conv": jax.random.normal(next(ks), (L, DEC_BATCH, CONV_HIST, D_CONV), f32),
        "cache_mem_k": jax.random.normal(next(ks), (L, DEC_BATCH, N_MEM, N_XHEADS, XHEAD_DIM), f32),
        "cache_mem_v": jax.random.normal(next(ks), (L, DEC_BATCH, N_MEM, N_XHEADS, XHEAD_DIM), f32),
        "g_ffn1": gain((L, D_MODEL)),
        "w_ffn1_in": nrm((L, D_MODEL, 2 * D_FF), D_MODEL),
        "w_ffn1_out": nrm((L, D_FF, D_MODEL), D_FF),
        "g_mix": gain((L, D_MODEL)),
        "w_mix_in": nrm((L, D_MODEL, D_POOL + 3 * D_CONV), D_MODEL),
        "pool_w": nrm((L, N_POOL_GROUPS, POOL_GROUP, POOL_GROUP), POOL_GROUP),
        "pool_scale": gain((L, D_POOL)),
        "conv_w": nrm((L, CONV_WIDTH, D_CONV), CONV_WIDTH),
        "w_mix_out": nrm((L, D_MIX, D_MODEL), D_MIX),
        "g_xq": gain((L, D_MODEL)),
        "g_mem": gain((L, D_MODEL)),
        "w_xq": nrm((L, D_MODEL, N_XHEADS * XHEAD_DIM), D_MODEL),
        "w_xk": nrm((L, D_MODEL, N_XHEADS * XHEAD_DIM), D_MODEL),
        "w_xv": nrm((L, D_MODEL, N_XHEADS * XHEAD_DIM), D_MODEL),
        "w_xo": nrm((L, N_XHEADS * XHEAD_DIM, D_MODEL), N_XHEADS * XHEAD_DIM),
        "g_ffn2": gain((L, D_MODEL)),
        "w_ffn2_in": nrm((L, D_MODEL, 2 * D_FF), D_MODEL),
        "w_ffn2_out": nrm((L, D_FF, D_MODEL), D_FF),
        "g_final": gain((D_MODEL,)),
    }


def reference(x_prompt, x_sample, mem_prompt, state_pool, state_conv, cache_mem_k, cache_mem_v,
              g_ffn1, w_ffn1_in, w_ffn1_out, g_mix, w_mix_in, pool_w, pool_scale, conv_w, w_mix_out,
              g_xq, g_mem, w_xq, w_xk, w_xv, w_xo, g_ffn2, w_ffn2_in, w_ffn2_out, g_final):
    b_p = x_prompt.shape[0]
    yp, ys = x_prompt, x_sample
    pp, pc, pk, pv, sp, sc = [], [], [], [], [], []
    for l in range(DEPTH):
        lw = (g_ffn1[l], w_ffn1_in[l], w_ffn1_out[l], g_mix[l], w_mix_in[l], pool_w[l], pool_scale[l],
              conv_w[l], w_mix_out[l], g_xq[l], w_xq[l], w_xo[l], g_ffn2[l], w_ffn2_in[l], w_ffn2_out[l])
        mn = _rms(mem_prompt, g_mem[l])
        mk = (mn @ w_xk[l]).reshape(b_p, N_MEM, N_XHEADS, XHEAD_DIM)
        mv = (mn @ w_xv[l]).reshape(b_p, N_MEM, N_XHEADS, XHEAD_DIM)
        zp_pool = jnp.zeros((b_p, POOL_HIST, D_POOL), yp.dtype)
        zp_conv = jnp.zeros((b_p, CONV_HIST, D_CONV), yp.dtype)
        yp, npool, nconv = _layer(yp, 0, zp_pool, zp_conv, mk, mv, lw)
        pp.append(npool); pc.append(nconv); pk.append(mk); pv.append(mv)
        ys, spool, sconv = _layer(ys, PAST_LEN, state_pool[l], state_conv[l], cache_mem_k[l], cache_mem_v[l], lw)
        sp.append(spool); sc.append(sconv)
    y_prompt = _rms(yp, g_final)
    y_sample = _rms(ys, g_final)
    return (y_prompt, y_sample, jnp.stack(pp), jnp.stack(pc), jnp.stack(pk), jnp.stack(pv), jnp.stack(sp), jnp.stack(sc))
```

```python
import numpy as np
from contextlib import ExitStack
import concourse.bass as bass
import concourse.mybir as mybir
from concourse.bass_utils import run_bass_kernel_spmd

F32 = mybir.dt.float32
BF16 = mybir.dt.bfloat16
AF = mybir.ActivationFunctionType
ALU = mybir.AluOpType

NCORES = 8
D = 1024
DFF = 2816
NTM = 1152
EPS = 1e-6
ENGS = ("pe", "act", "dve", "pool", "sp")
SAME_ENG_WAR = True
SAME_ENG_WAW = False
INC_NORM = True
WAIT_COUNT = {}


class _Op:
    __slots__ = ("eng", "fn", "deps", "signal", "dma_key", "dma_idx", "sig_val", "group")

    def __init__(self, eng, fn, dma_key):
        self.group = None
        self.eng = eng
        self.fn = fn
        self.deps = []
        self.signal = False
        self.dma_key = dma_key
        self.dma_idx = 0
        self.sig_val = 0


class Prog:
    def __init__(self):
        self.ops = {e: [] for e in ENGS}
        self.last_w = {}
        self.readers = {}
        self.dma_count = {}
        self.wait_all_keys = set()

    def add(self, eng, fn, reads=(), writes=(), dma_key=None, group=None):
        o = _Op(eng, fn, dma_key)
        o.group = group
        is_dma = dma_key is not None
        deps = {}
        for r in reads:
            w = self.last_w.get(r)
            if w is not None:
                deps[id(w)] = w
        for w_ in writes:
            lw = self.last_w.get(w_)
            if lw is not None and (SAME_ENG_WAW or is_dma or lw.dma_key is not None or lw.eng != eng):
                deps[id(lw)] = lw
            for rd in self.readers.get(w_, {}).values():
                if SAME_ENG_WAR or is_dma or rd.dma_key is not None or rd.eng != eng:
                    deps[id(rd)] = rd
        for d in deps.values():
            if d is o or (group is not None and d.group == group):
                continue
            if d.dma_key is None and not is_dma and d.eng == "pe" and eng == "pe":
                continue
            o.deps.append(d)
            d.signal = True
        stream = ("dma", dma_key) if is_dma else eng
        for r in reads:
            self.readers.setdefault(r, {})[stream] = o
        for w_ in writes:
            self.last_w[w_] = o
            self.readers[w_] = {}
        if is_dma:
            self.dma_count[dma_key] = self.dma_count.get(dma_key, 0) + 1
            o.dma_idx = self.dma_count[dma_key]
        self.ops[eng].append(o)
        return o

    def finalize(self):
        for e in ENGS:
            c = 0
            for o in self.ops[e]:
                if o.dma_key is None and o.signal:
                    c += 1
                    o.sig_val = c

    def emit(self, eng, handle, esem, dsem, final_wait=False):
        waited = {}
        for o in self.ops[eng]:
            for d in o.deps:
                if d.dma_key is not None:
                    sem = dsem[d.dma_key]
                    if d.dma_key in self.wait_all_keys:
                        val = 16 * self.dma_count[d.dma_key]
                    else:
                        val = 16 * d.dma_idx
                else:
                    sem = esem[d.eng]
                    val = d.sig_val
                key = id(sem)
                if waited.get(key, 0) < val:
                    handle.wait_ge(sem, val)
                    waited[key] = val
                    WAIT_COUNT[eng] = WAIT_COUNT.get(eng, 0) + 1
            ins = o.fn(handle)
            if o.dma_key is not None:
                ins.then_inc(dsem[o.dma_key], 16)
            elif o.signal:
                ins.then_inc(esem[eng], 1)
        if final_wait:
            for k, cnt in self.dma_count.items():
                handle.wait_ge(dsem[k], 16 * cnt)


def build_program(dbg_stage=None, skip=()):
    nc = bass.Bass("TRN2", target_bir_lowering=False)
    P = Prog()

    def din(name, shape, dt=F32):
        return nc.dram_tensor(name, list(shape), dt, kind="ExternalInput").ap()

    def dout(name, shape, dt=F32):
        return nc.dram_tensor(name, list(shape), dt, kind="ExternalOutput").ap()

    x_p = din("x_p", [2048, D]); x_s = din("x_s", [128, D]); mem = din("mem", [256, D])
    st_pool = din("st_pool", [240, 512]); st_conv = din("st_conv", [32, 512])
    ck = din("ck", [4096, D]); cv = din("cv", [4096, D])
    w_f1i = din("w_f1i", [D, 2 * DFF]); w_f1o = din("w_f1o", [DFF, D])
    w_f2i = din("w_f2i", [D, 2 * DFF]); w_f2o = din("w_f2o", [DFF, D])
    w_mi = din("w_mi", [D, 2048]); w_mo = din("w_mo", [D, D])
    w_xq = din("w_xq", [D, D]); w_xk = din("w_xk", [D, D]); w_xv = din("w_xv", [D, D]); w_xo = din("w_xo", [D, D])
    pool_w = din("pool_w", [512, 128])
    gains_d = din("gains", [128, 48])
    psc_d = din("pool_scale", [128, 4]); cw_d = din("conv_w", [128, 12])
    c_ident = din("c_ident", [128, 128]); c_invc = din("c_invc", [128, 64])

    y_p = dout("y_p", [2048, D]); y_s = dout("y_s", [128, D])
    np_p = dout("np_p", [15, 512]); nc_p = dout("nc_p", [2, 512])
    nk_p = dout("nk_p", [256, D]); nv_p = dout("nv_p", [256, D])
    np_s = dout("np_s", [240, 512]); nc_s = dout("nc_s", [32, 512])
    dbg = dout("dbg", [128, 8 * NTM]) if dbg_stage is not None else None

    es = ExitStack()

    def sb(name, shape, dt):
        return es.enter_context(nc.sbuf_tensor(name, list(shape), dt))

    xT = sb("xT", [128, 8, NTM], F32)
    hT = sb("hT", [128, 8, NTM], BF16)
    U = sb("U", [128, 8, NTM], BF16)
    wsl = [sb(f"wsl{i}", [128, 4096], BF16) for i in range(4)]
    sq = sb("sq", [128, 8, 512], BF16)
    rstd = [sb(f"rstd{i}", [128, 512], F32) for i in range(2)]
    sg = [sb(f"sg{i}", [128, 512], F32) for i in range(2)]
    NIO = 4
    io = [sb(f"io{i}", [128, 1024], F32) for i in range(NIO)]
    scr = sb("scr", [128, 3 * 1408], F32)
    scr2 = sb("scr2", [128, 3 * 1408], F32)
    scr_bf = scr.bitcast(BF16)
    poolhT = sb("poolhT", [128, 4, 240], F32)
    npsT = sb("npsT", [128, 4, 240], F32)
    convhT = sb("convhT", [128, 4, 32], F32)
    ncsT = sb("ncsT", [128, 4, 32], F32)
    kTp = sb("kTp", [128, 8, 256], BF16)
    Vp = sb("Vp", [128, 2, 1024], BF16)
    qTs = sb("qTs", [128, 8, 128], BF16)
    oTs = sb("oTs", [128, 8, 128], BF16)
    ET = [sb(f"ET{i}", [128, 2, 512], BF16) for i in range(2)]
    gains = sb("gains_sb", [128, 6, 8], F32)
    psc = sb("psc", [128, 4], F32)
    cw = sb("cw", [128, 12], F32)
    ones_n = sb("ones_n", [128, 128], BF16)
    ones1 = sb("ones1", [128, 128], BF16)
    identf = sb("identf", [128, 128], F32)
    identb = sb("identb", [128, 128], BF16)
    poolw = sb("poolw", [128, 4, 128], BF16)
    ahist = sb("ahist", [128, 4, 16], F32)
    uhist = sb("uhist", [128, 4, 2], F32)
    invc = sb("invc", [128, 4, 16], F32)
    ETs = sb("ETs", [128, 64], BF16)
    ETsB = sb("ETsB", [128, 64], BF16)
    rdens = sb("rdens", [128, 32], F32)
    tmp16 = sb("tmp16", [128, 16], F32)
    stg = [sb(f"stg{i}", [128, 512], F32) for i in range(2)]
    zero15 = sb("zero15", [128, 16], F32)

    ps = [es.enter_context(nc.psum_tensor(f"ps{i}", [128, 512], F32)) for i in range(8)]
    ps_bf = [p_.bitcast(BF16) for p_ in ps]

    A0 = scr[:, 0:1408]; S1 = scr[:, 1408:2816]; S2 = scr[:, 2816:4224]
    Knat = scr_bf[:, 0:2048]; Vnat = scr_bf[:, 2816:2816 + 2048]; kTs = scr_bf[:, 5632:5632 + 2048]
    yT = scr[:, 0:4096]

    bank_ctr = [0]
    sg_ctr = [0]

    pinned = set()

    def nb():
        while True:
            b = bank_ctr[0] % 8
            bank_ctr[0] += 1
            if b not in pinned:
                return b

    def BK(b):
        return ("ps", b)

    def mm_group(out_ap, pairs, reads, b):
        n = len(pairs)
        for i, (l, r) in enumerate(pairs):
            P.add("pe", (lambda e, l=l, r=r, i=i: e.matmul(out_ap, lhsT=l, rhs=r, start=(i == 0), stop=(i == n - 1))),
                  reads=reads, writes=[BK(b)])

    def transp(out_ap, in_ap, ident_ap, reads, b):
        P.add("pe", (lambda e: e.transpose(out_ap, in_ap, ident_ap)), reads=reads + ["c_identf", "c_identb"], writes=[BK(b)])

    P.wait_all_keys.add("init")

    def init_load(out_ap, in_ap, res, eng="sp"):
        P.add(eng, (lambda e: e.dma_start(out=out_ap, in_=in_ap)), writes=[res], dma_key=("init" if eng == "sp" else "initp"))

    init_load(identf[:], c_ident[:, :], "c_identf")
    init_load(invc[:], c_invc.rearrange("p (g c) -> p g c", g=4), "c_invc")
    init_load(gains[:], gains_d.rearrange("p (g k) -> p g k", g=6), "c_gains")
    init_load(psc[:], psc_d[:, :], "c_psc")
    init_load(cw[:], cw_d[:, :], "c_cw")
    init_load(poolw[:], pool_w.rearrange("(g c) d -> c g d", g=4), "c_poolw", eng="pool")
    P.add("dve", lambda e: e.memset(ones_n[:], 1.0 / 1024.0), writes=["c_ones_n"])
    P.add("dve", lambda e: e.memset(ones1[:], 1.0), writes=["c_ones1"])
    P.add("dve", lambda e: e.memset(zero15[:], 0.0), writes=["c_zero"])
    P.add("dve", lambda e: e.tensor_copy(out=identb[:], in_=identf[:]), reads=["c_identf"], writes=["c_identb"])

    wblocks = []
    wb_emitted = [0]

    def wblock(parts):
        wblocks.append(parts)
        return len(wblocks) - 1

    def slot_view(slot, kk, width):
        return wsl[slot][:, 0:kk * width].rearrange("p (k n) -> p k n", k=kk)

    slot_content = {}

    def slot_view_blk(blk, kk, width):
        assert slot_content.get(blk % 4) == blk, ("weight slot no longer holds block", blk, slot_content)
        return slot_view(blk % 4, kk, width)

    def prefetch(upto):
        upto = min(upto, len(wblocks) - 1)
        while wb_emitted[0] <= upto:
            b = wb_emitted[0]
            slot = b % 4
            slot_content[slot] = b
            batch = []
            for pi, (kk, width, c0, n, src) in enumerate(wblocks[b]):
                dst = slot_view(slot, kk, width)[:, :, c0:c0 + n]
                batch.append(P.add("pool", (lambda e, dst=dst, src=src: e.dma_start(out=dst, in_=src.rearrange("(k p) n -> p k n", p=128))),
                                   writes=[("w", slot, q) for q in range(3)], dma_key=f"w{slot}", group=("wb", b)))
            for o_ in batch:
                o_.dma_idx = batch[-1].dma_idx
            wb_emitted[0] += 1

    def wres(b, npart):
        return [("w", b % 4, pi) for pi in range(npart)]

    sched = {}

    def ffn_blocks(tag, w_in, w_out, inserts=None):
        thirds = [(0, 8), (8, 8), (16, 6)]
        out = []
        n_in = 0
        for (j0, kh) in thirds:
            ins = []
            for jj in range(0, kh, 2):
                j = j0 + jj
                ins.append(wblock([(8, 512, 0, 256, w_in[:, j * 128:(j + 2) * 128]),
                                   (8, 512, 256, 256, w_in[:, DFF + j * 128:DFF + (j + 2) * 128])]))
                if inserts is not None and n_in in inserts:
                    sq_blocks(*inserts[n_in])
                n_in += 1
            outs = [wblock([(kh, 512, 0, 512, w_out[j0 * 128:(j0 + kh) * 128, cb * 512:(cb + 1) * 512])]) for cb in range(2)]
            out.append((j0, kh, ins, outs))
        sched[tag] = out

    def sq_blocks(tag, w):
        sched[tag] = [wblock([(8, 512, 0, 512, w[:, cb * 512:(cb + 1) * 512])]) for cb in range(2)]

    def mix_blocks(tag):
        blks = [wblock([(8, 512, 0, 512, w_mi[:, 512 * i:512 * (i + 1)])]) for i in range(4)]
        sched[tag] = (blks[0], blks[1:])

    for pss in range(2):
        ffn_blocks(("f1", pss), w_f1i, w_f1o, inserts=({1: ("xk", w_xk), 2: ("xv", w_xv)} if pss == 0 else None))
        mix_blocks(("mi", pss))
        sq_blocks(("mo", pss), w_mo)
        sq_blocks(("xq", pss), w_xq)
        sq_blocks(("xo", pss), w_xo)
        ffn_blocks(("f2", pss), w_f2i, w_f2o)

    def XR(si, k):
        return ("x", si, k)

    rs_ctr = [0]

    SQ_ALL = [("sq", o_) for o_ in range(8)]

    def norm_sub(src, sub, gi, dst, bss=None):
        (si, c0, n) = sub
        xr = [XR(si, k) for k in range(8)]
        if bss is None:
            P.add("act", (lambda e: e.activation(out=sq[:, :, 0:n], in_=src[:, :, c0:c0 + n], func=AF.Square)),
                  reads=xr, writes=SQ_ALL)
            b = nb()
            mm_group(ps[b][:, 0:n], [(ones_n[:], sq[:, k, 0:n]) for k in range(8)], SQ_ALL + ["c_ones_n"], b)
        else:
            b = bss
        ri = rs_ctr[0] % 2
        rs_ctr[0] += 1
        r = rstd[ri]
        rr = ("rstd", ri)
        P.add("act", (lambda e: e.activation(out=r[:, 0:n], in_=ps[b][:, 0:n], func=AF.Ln, bias=EPS, scale=1.0)),
              reads=[BK(b)], writes=[rr])
        P.add("act", (lambda e: e.activation(out=r[:, 0:n], in_=r[:, 0:n], func=AF.Exp, scale=-0.5)), reads=[rr], writes=[rr])
        if bss is not None:
            pinned.discard(bss)
        return r, rr, xr

    def norm_apply(src, sub, gi, dst, r, rr, xr, dst_res):
        (si, c0, n) = sub
        for k in range(8):
            P.add("dve", (lambda e, k=k: e.scalar_tensor_tensor(
                out=dst(k, c0, n), in0=src[:, k, c0:c0 + n], scalar=gains[:, gi, k:k + 1], in1=r[:, 0:n],
                op0=ALU.mult, op1=ALU.mult)),
                reads=[xr[k], rr, "c_gains"], writes=dst_res(si))

    def norm(src, subs, gi, dst_t):
        for sub in subs:
            r, rr, xr = norm_sub(src, sub, gi, dst_t)
            norm_apply(src, sub, gi, (lambda k, c0, n: dst_t[:, k, c0:c0 + n]), r, rr, xr, (lambda si: [("hT", si)]))

    def proj_accum(blocks, subs, rhs_of, rhs_res_of, half_scale, kk=8, after=None, sub_hook=None, after_early=False):
        pending = []
        inc = after is not None and INC_NORM
        for (si, c0, n) in subs:
            bss = None
            if inc:
                bss = nb()
                pinned.add(bss)
            ssq = []

            def ss_mm(o_, bss=bss, n=n):
                P.add("pe", (lambda e: e.matmul(ps[bss][:, 0:n], lhsT=ones_n[:], rhs=sq[:, o_, 0:n], start=(o_ == 0), stop=(o_ == 7))),
                      reads=[("sq", o_), "c_ones_n"], writes=[BK(bss)])
            for o in range(8):
                blk = blocks[o // 4]
                sv = slot_view_blk(blk, kk, 512)
                b = nb()
                mm_group(ps[b][:, 0:n], [(sv[:, k, (o % 4) * 128:(o % 4 + 1) * 128], rhs_of(si, c0, n, k)) for k in range(kk)],
                         wres(blk, 1) + rhs_res_of(si), b)
                if half_scale:
                    P.add("dve", (lambda e, b=b, o=o, c0=c0, n=n: e.scalar_tensor_tensor(
                        out=xT[:, o, c0:c0 + n], in0=ps[b][:, 0:n], scalar=0.5, in1=xT[:, o, c0:c0 + n], op0=ALU.mult, op1=ALU.add)),
                        reads=[BK(b), XR(si, o)], writes=[XR(si, o)])
                else:
                    P.add("dve", (lambda e, b=b, o=o, c0=c0, n=n: e.tensor_tensor(
                        out=xT[:, o, c0:c0 + n], in0=ps[b][:, 0:n], in1=xT[:, o, c0:c0 + n], op=ALU.add)),
                        reads=[BK(b), XR(si, o)], writes=[XR(si, o)])
                if inc:
                    P.add("act", (lambda e, o=o, c0=c0, n=n: e.activation(out=sq[:, o, 0:n], in_=xT[:, o, c0:c0 + n], func=AF.Square)),
                          reads=[XR(si, o)], writes=[("sq", o)])
                    ssq.append(o)
                    if len(ssq) > 4:
                        ss_mm(ssq.pop(0))
                if after is not None and after_early and pending and o == 2:
                    after(*pending.pop())
            while ssq:
                ss_mm(ssq.pop(0))
            if sub_hook is not None:
                sub_hook()
            if after is not None:
                if pending:
                    after(*pending.pop())
                pending.append(((si, c0, n), bss))
        if after is not None and pending:
            after(*pending.pop())

    def ffn(tag, gi, subs, hooks=None, prenormed=False, after=None, pre_last=None, blk_hooks=None, after_early=True):
        if not prenormed:
            norm(xT, subs, gi, hT)
        hook_st = [0]
        blk_i = 0

        def one_hook():
            if hooks is not None and hook_st[0] < len(hooks):
                hooks[hook_st[0]]()
                hook_st[0] += 1

        one_hook()
        def in_piece(blk, bi, jj, sub):
            (si, c0, n) = sub
            sv = slot_view_blk(blk, 8, 512)
            jl = bi * 2 + jj
            bg = nb(); bu = nb()
            mm_group(ps[bg][:, 0:n], [(sv[:, k, jj * 128:(jj + 1) * 128], hT[:, k, c0:c0 + n]) for k in range(8)],
                     [("w", blk % 4, 0), ("hT", si)], bg)
            mm_group(ps[bu][:, 0:n], [(sv[:, k, 256 + jj * 128:256 + (jj + 1) * 128], hT[:, k, c0:c0 + n]) for k in range(8)],
                     [("w", blk % 4, 1), ("hT", si)], bu)
            sgi = sg_ctr[0] % 2; sg_ctr[0] += 1
            s_ = sg[sgi]
            P.add("act", (lambda e: e.activation(out=s_[:, 0:n], in_=ps[bg][:, 0:n], func=AF.Silu)),
                  reads=[BK(bg)], writes=[("sg", sgi)])
            P.add("dve", (lambda e: e.tensor_tensor(out=U[:, jl, c0:c0 + n], in0=ps[bu][:, 0:n], in1=s_[:, 0:n], op=ALU.mult)),
                  reads=[BK(bu), ("sg", sgi)], writes=[("U", si)])

        for (j0, kh, ins, outs) in sched[tag]:
            bi = 0
            while bi < len(ins):
                if j0 == 0 and bi == 0 and len(ins) >= 2:
                    prefetch(ins[1] + 2)
                    for s_i, sub in enumerate(subs):
                        for b2 in range(2):
                            for jj in range(2):
                                in_piece(ins[b2], b2, jj, sub)
                        if s_i < 2:
                            one_hook()
                    blk_i += 2
                    bi += 2
                    if blk_hooks is not None and 1 in blk_hooks:
                        blk_hooks[1]()
                    continue
                blk = ins[bi]
                prefetch(blk + 3)
                for jj in range(2):
                    for sub in subs:
                        in_piece(blk, bi, jj, sub)
                blk_i += 1
                one_hook()
                if blk_hooks is not None and j0 == 0 and bi in blk_hooks:
                    blk_hooks[bi]()
                bi += 1
            prefetch(outs[1] + 2)
            if j0 == 16 and pre_last is not None:
                pre_last()
            proj_accum(outs, subs, (lambda si, c0, n, k: U[:, k, c0:c0 + n]), (lambda si: [("U", si)]), True, kk=kh,
                       after=(after if j0 == 16 else None), sub_hook=one_hook, after_early=after_early)
        assert hooks is None or hook_st[0] == len(hooks), "sample-attention hooks must finish before the epilogue reuses scr"

    io_ctr = [0]

    def load_tokens(src_rows_ap, col0, si, dst3=None, res_of=None):
        dst3 = xT if dst3 is None else dst3
        res_of = (lambda k: XR(si, k)) if res_of is None else res_of
        i = io_ctr[0] % NIO
        io_ctr[0] += 1
        buf = io[i]
        P.add("sp", (lambda e: e.dma_start(out=buf[:], in_=src_rows_ap)), writes=[("io", i)], dma_key=f"io{i}")
        for half in range(2):
            b = nb()
            for j in range(4):
                k = half * 4 + j
                transp(ps[b][:, j * 128:(j + 1) * 128], buf[:, k * 128:(k + 1) * 128], identf[:], [("io", i)], b)
            dstv = dst3[:, half * 4:half * 4 + 4, col0:col0 + 128]
            srcv = ps[b][:, :].rearrange("p (j t) -> p j t", j=4)
            eng = "act" if half == 0 else "dve"
            if eng == "act":
                P.add("act", (lambda e, dstv=dstv, srcv=srcv: e.activation(out=dstv, in_=srcv, func=AF.Copy)),
                      reads=[BK(b)], writes=[res_of(half * 4 + j) for j in range(4)])
            else:
                P.add("dve", (lambda e, dstv=dstv, srcv=srcv: e.tensor_copy(out=dstv, in_=srcv)),
                      reads=[BK(b)], writes=[res_of(half * 4 + j) for j in range(4)])

    hT_f = hT.bitcast(F32)
    U_f = U.bitcast(F32)
    xstage = {0: (hT_f[:, :, :].rearrange("p k t -> p (k t)")[:, 0:4096].rearrange("p (c n) -> p c n", c=4), [("hT", q) for q in range(3)], "xsA"),
              1: (U_f[:, :, :].rearrange("p k t -> p (k t)")[:, 0:4096].rearrange("p (c n) -> p c n", c=4), [("U", q) for q in range(3)], "xsB")}
    P.wait_all_keys.add("xsA")
    P.wait_all_keys.add("xsB")

    def stage_loads(si):
        view3, res, key = xstage[si]
        for c in range(4):
            r0 = 1024 + si * 512 + c * 128
            P.add("sp", (lambda e, c=c, r0=r0: e.dma_start(out=view3[:, c, :], in_=x_p[r0:r0 + 128, :])),
                  writes=res, dma_key=key, group=key)

    def staged_to_xT(si):
        view3, res, key = xstage[si]
        for c in range(4):
            col0 = si * 512 + c * 128
            for half in range(2):
                b = nb()
                for j in range(4):
                    k = half * 4 + j
                    transp(ps[b][:, j * 128:(j + 1) * 128], view3[:, c, k * 128:(k + 1) * 128], identf[:], res, b)
                dstv = xT[:, half * 4:half * 4 + 4, col0:col0 + 128]
                srcv = ps[b][:, :].rearrange("p (j t) -> p j t", j=4)
                if half == 0:
                    P.add("act", (lambda e, dstv=dstv, srcv=srcv: e.activation(out=dstv, in_=srcv, func=AF.Copy)),
                          reads=[BK(b)], writes=[XR(si, half * 4 + j) for j in range(4)])
                else:
                    P.add("dve", (lambda e, dstv=dstv, srcv=srcv: e.tensor_copy(out=dstv, in_=srcv)),
                          reads=[BK(b)], writes=[XR(si, half * 4 + j) for j in range(4)])

    SUB_P = [(0, 0, 512), (1, 512, 512)]
    SUB_S = [(2, 1024, 128)]

    early_x = all(k not in skip for k in ("kv", "load"))
    prefetch(3)
    if "states" not in skip:
        for half in range(2):
            i = io_ctr[0] % NIO
            io_ctr[0] += 1
            P.add("sp", (lambda e, i=i, half=half: e.dma_start(out=io[i][0:120, 0:512], in_=st_pool[half * 120:(half + 1) * 120, :])),
                  writes=[("io", i)], dma_key=f"io{i}")
            b = nb()
            for g in range(4):
                transp(ps[b][:, g * 120:(g + 1) * 120], io[i][0:120, g * 128:(g + 1) * 128], identf[0:120, 0:120], [("io", i)], b)
            P.add("act", (lambda e, b=b, half=half: e.activation(
                out=poolhT[:, :, half * 120:(half + 1) * 120], in_=ps[b][:, 0:480].rearrange("p (g r) -> p g r", g=4), func=AF.Copy)),
                reads=[BK(b)], writes=["poolhT"])
        i = io_ctr[0] % NIO
        io_ctr[0] += 1
        P.add("sp", (lambda e, i=i: e.dma_start(out=io[i][0:32, 0:512], in_=st_conv[:, :])), writes=[("io", i)], dma_key=f"io{i}")
        b = nb()
        for g in range(4):
            transp(ps[b][:, g * 32:(g + 1) * 32], io[i][0:32, g * 128:(g + 1) * 128], identf[0:32, 0:32], [("io", i)], b)
        P.add("act", (lambda e, b=b: e.activation(out=convhT[:, :, :], in_=ps[b][:, 0:128].rearrange("p (g r) -> p g r", g=4), func=AF.Copy)),
              reads=[BK(b)], writes=["convhT"])

    if "kv" not in skip:
        memT = scr[:, 0:2048].rearrange("p (k t) -> p k t", k=8)
        for mc in range(2):
            load_tokens(mem[mc * 128:(mc + 1) * 128, :], mc * 128, 0, dst3=memT, res_of=(lambda k: "R0"))
        if early_x:
            for tcn in range(8):
                load_tokens(x_p[tcn * 128:(tcn + 1) * 128, :], tcn * 128, tcn // 4)
            load_tokens(x_s[:, :], 1024, 2)
        mnT = scr_bf[:, 5632:5632 + 2048].rearrange("p (k t) -> p k t", k=8)
        if "kv_norm" not in skip:
            P.add("act", (lambda e: e.activation(out=sq[:, :, 0:256], in_=memT[:, :, :], func=AF.Square)), reads=["R0"], writes=SQ_ALL)
            b = nb()
            mm_group(ps[b][:, 0:256], [(ones_n[:], sq[:, k, 0:256]) for k in range(8)], SQ_ALL + ["c_ones_n"], b)
            r = rstd[0]
            P.add("act", (lambda e, b=b, r=r: e.activation(out=r[:, 0:256], in_=ps[b][:, 0:256], func=AF.Ln, bias=EPS, scale=1.0)),
                  reads=[BK(b)], writes=[("rstd", 0)])
            P.add("act", (lambda e, r=r: e.activation(out=r[:, 0:256], in_=r[:, 0:256], func=AF.Exp, scale=-0.5)), reads=[("rstd", 0)], writes=[("rstd", 0)])
            for k in range(8):
                P.add("dve", (lambda e, k=k, r=r: e.scalar_tensor_tensor(
                    out=mnT[:, k, :], in0=memT[:, k, :], scalar=gains[:, 3, k:k + 1], in1=r[:, 0:256], op0=ALU.mult, op1=ALU.mult)),
                    reads=["R0", ("rstd", 0), "c_gains"], writes=["R2"])
    stg_ctr = [0]

    def kv_tok(wb, dst_d, is_v):
        for mc in range(2):
            for cb in range(2):
                blk = wb[cb]
                sv = slot_view_blk(blk, 8, 512)
                b = nb()
                mm_group(ps[b][:, :], [(mnT[:, k, mc * 128:(mc + 1) * 128], sv[:, k, :]) for k in range(8)],
                         wres(blk, 1) + ["R2"], b)
                i = stg_ctr[0] % 2
                stg_ctr[0] += 1
                P.add("act", (lambda e, b=b, i=i: e.activation(out=stg[i][:, :], in_=ps[b][:, :], func=AF.Copy)),
                      reads=[BK(b)], writes=[("stg", i)])
                if is_v:
                    P.add("dve", (lambda e, i=i, mc=mc, cb=cb: e.tensor_copy(out=Vp[:, mc, cb * 512:(cb + 1) * 512], in_=stg[i][:, :])),
                          reads=[("stg", i)], writes=["Vp"])
                P.add("sp", (lambda e, i=i, mc=mc, cb=cb: e.dma_start(
                    out=dst_d[mc * 128:(mc + 1) * 128, cb * 512:(cb + 1) * 512], in_=stg[i][:, :])),
                    reads=[("stg", i)], dma_key=f"stg{i}")

    def kv_hook_k():
        xk = sched["xk"]
        prefetch(xk[1] + 2)
        for oc in range(8):
            blk = xk[oc // 4]
            sv = slot_view_blk(blk, 8, 512)
            b = nb()
            mm_group(ps[b][:, 0:256], [(sv[:, k, (oc % 4) * 128:(oc % 4 + 1) * 128], mnT[:, k, :]) for k in range(8)],
                     wres(blk, 1) + ["R2"], b)
            P.add("act", (lambda e, b=b, oc=oc: e.activation(out=kTp[:, oc, :], in_=ps[b][:, 0:256], func=AF.Copy)),
                  reads=[BK(b)], writes=["kTp"])
        kv_tok(xk, nk_p, False)

    def kv_hook_v():
        xv = sched["xv"]
        prefetch(xv[1] + 2)
        kv_tok(xv, nv_p, True)

    def dbg_dump(stage):
        if dbg_stage == stage:
            P.add("sp", (lambda e: e.dma_start(out=dbg[:, :], in_=xT[:, :, :].rearrange("p k t -> p (k t)"))),
                  reads=[XR(si, k) for si in range(3) for k in range(8)], dma_key="dbg")

    ETs2 = [ETs, ETsB]
    kT3 = kTs.rearrange("p (c m) -> p c m", c=8)

    def sa_load_k(i):
        P.add("pool", (lambda e: e.dma_start(out=Knat.rearrange("p (mc n) -> p mc n", mc=2),
                                             in_=ck[i * 256:(i + 1) * 256, :].rearrange("(mc p) n -> p mc n", p=128))),
              writes=["R0"], dma_key="kn")

    def sa_load_v(i):
        P.add("pool", (lambda e: e.dma_start(out=Vnat.rearrange("p (mc n) -> p mc n", mc=2),
                                             in_=cv[i * 256:(i + 1) * 256, :].rearrange("(mc p) n -> p mc n", p=128))),
              writes=["R1"], dma_key="vn")

    def sa_s1(i):
        for mc in range(2):
            for quad in range(2):
                b = nb()
                for j in range(4):
                    c = quad * 4 + j
                    transp(ps_bf[b][:, j * 128:(j + 1) * 128], Knat[:, mc * 1024 + c * 128: mc * 1024 + (c + 1) * 128], identb[:], ["R0"], b)
                dstv = kT3[:, quad * 4:quad * 4 + 4, mc * 128:(mc + 1) * 128]
                srcv = ps_bf[b][:, 0:512].rearrange("p (j m) -> p j m", j=4)
                if quad == 0:
                    P.add("act", (lambda e, dstv=dstv, srcv=srcv: e.activation(out=dstv, in_=srcv, func=AF.Copy)), reads=[BK(b)], writes=["R2"])
                else:
                    P.add("dve", (lambda e, dstv=dstv, srcv=srcv: e.tensor_copy(out=dstv, in_=srcv)), reads=[BK(b)], writes=["R2"])

    def sa_s2(i):
        b = nb()
        Eb = ETs2[i % 2]
        for h in range(4):
            for mc in range(2):
                col = (h * 2 + mc) * 8
                mm_group(ps[b][:, col:col + 8],
                         [(kT3[:, h * 2 + dc, mc * 128:(mc + 1) * 128], qTs[:, h * 2 + dc, i * 8:(i + 1) * 8]) for dc in range(2)],
                         ["R2", "qTs"], b)
        P.add("act", (lambda e: e.activation(out=Eb[:, :], in_=ps[b][:, 0:64], func=AF.Exp, scale=1.0 / 16.0)),
              reads=[BK(b)], writes=[("ETs", i % 2)])

    def sa_s3(i):
        Eb = ETs2[i % 2]
        er = ("ETs", i % 2)
        bd = nb()
        E4 = Eb[:, :].rearrange("p (h mc t) -> p h mc t", h=4, mc=2)
        mm_group(ps[bd][:, 0:32].rearrange("p (h t) -> p h t", h=4), [(ones1[:], E4[:, :, mc, :]) for mc in range(2)], [er, "c_ones1"], bd)
        P.add("dve", (lambda e: e.reciprocal(out=rdens[:, :], in_=ps[bd][:, 0:32])), reads=[BK(bd)], writes=["rdens"])
        bo = nb()
        for h in range(4):
            for dc in range(2):
                c = h * 2 + dc
                mm_group(ps[bo][:, c * 8:(c + 1) * 8],
                         [(Vnat[:, mc * 1024 + c * 128: mc * 1024 + (c + 1) * 128], E4[:, h, mc, :]) for mc in range(2)],
                         ["R1", er], bo)
        for dc in range(2):
            o4 = oTs[:, :, i * 8:(i + 1) * 8].rearrange("p (h dc) t -> p h dc t", dc=2)[:, :, dc, :]
            p4 = ps[bo][:, 0:64].rearrange("p (h dc t) -> p h dc t", h=4, dc=2)[:, :, dc, :]
            r4 = rdens[:, :].rearrange("p (h t) -> p h t", h=4)
            P.add("dve", (lambda e, o4=o4, p4=p4, r4=r4: e.tensor_tensor(out=o4, in0=p4, in1=r4, op=ALU.mult)),
                  reads=[BK(bo), "rdens"], writes=["oTs"])

    def sa_hook(t):
        if 0 <= t - 2 < 16:
            sa_s3(t - 2)
        if 0 <= t - 1 < 16:
            sa_load_v(t - 1)
            sa_s2(t - 1)
        if t < 16:
            sa_s1(t)
            if t + 1 < 16:
                sa_load_k(t + 1)

    yT3 = yT.rearrange("p (k t) -> p k t", k=8)
    cur_pass = [0]

    yT3b = scr2[:, 0:4096].rearrange("p (k t) -> p k t", k=8)

    def final_norm(sub, which=0, bss=None):
        ybuf = yT3 if which == 0 else yT3b
        yres = ["R0", "R1", "R2"] if which == 0 else ["Q0", "Q1", "Q2"]
        r, rr, xr = norm_sub(xT, sub, 5, None, bss=bss)
        norm_apply(xT, sub, 5, (lambda k, c0_, n_: ybuf[:, k, 0:n_]), r, rr, xr, (lambda si_: yres))

    def final_out_chunk(sub, tcn, pss_, which=0):
        (si, c0, n) = sub
        ybuf = yT3 if which == 0 else yT3b
        yres = ["R0", "R1", "R2"] if which == 0 else ["Q0", "Q1", "Q2"]
        i = io_ctr[0] % NIO
        io_ctr[0] += 1
        for half in range(2):
            b = nb()
            for j in range(4):
                k = half * 4 + j
                transp(ps[b][:, j * 128:(j + 1) * 128], ybuf[:, k, tcn * 128:(tcn + 1) * 128], identf[:], yres, b)
            if half == 0:
                P.add("act", (lambda e, b=b, i=i: e.activation(out=io[i][:, 0:512], in_=ps[b][:, :], func=AF.Copy)), reads=[BK(b)], writes=[("io", i)])
            else:
                P.add("dve", (lambda e, b=b, i=i: e.tensor_copy(out=io[i][:, 512:1024], in_=ps[b][:, :])), reads=[BK(b)], writes=[("io", i)])
        if si < 2:
            r0 = pss_ * 1024 + c0 + tcn * 128
            dst = y_p[r0:r0 + 128, :]
        else:
            dst = y_s[:, :]
        P.add("sp", (lambda e, i=i, dst=dst: e.dma_start(out=dst, in_=io[i][:, :])), reads=[("io", i)], dma_key=f"io{i}")

    def final_sub(sub, bss=None):
        final_norm(sub, 0, bss=bss)
        for tcn in range(sub[2] // 128):
            final_out_chunk(sub, tcn, cur_pass[0], 0)

    def next_norm(gi):
        def f(sub, bss=None):
            r, rr, xr = norm_sub(xT, sub, gi, hT, bss=bss)
            norm_apply(xT, sub, gi, (lambda k, c0, n: hT[:, k, c0:c0 + n]), r, rr, xr, (lambda si: [("hT", si)]))
        return f

    preload_next = all(k not in skip for k in ("final", "ffn2", "load", "ffn1"))
    deferred_out = []
    for pss in range(2):
        cur_pass[0] = pss
        subs_a = (SUB_S if pss == 0 else []) + SUB_P
        subs_b = (SUB_S if pss == 1 else []) + SUB_P
        if "load" not in skip and not (pss == 1 and preload_next) and not (pss == 0 and early_x):
            for tcn in range(8):
                r0 = pss * 1024 + tcn * 128
                load_tokens(x_p[r0:r0 + 128, :], tcn * 128, tcn // 4)
            if pss == 0:
                load_tokens(x_s[:, :], 1024, 2)
        dbg_dump(("load", pss))

        if "ffn1" not in skip:
            ffn(("f1", pss), 0, subs_a, prenormed=(pss == 1 and preload_next), after=(next_norm(1) if "mix" not in skip else None),
                hooks=(deferred_out if (pss == 1 and deferred_out) else None),
                blk_hooks=({1: kv_hook_k, 2: kv_hook_v} if (pss == 0 and "kv" not in skip) else None))
        dbg_dump(("ffn1", pss))

        if "mix" not in skip:
            if "ffn1" in skip:
                norm(xT, subs_a, 1, hT)
            ablk, heads = sched[("mi", pss)]
            prefetch(ablk + 3)
            sva = slot_view_blk(ablk, 8, 512)
            has_s = (pss == 0)
            DTf_all = sq[:, :, :].rearrange("p k t -> p (k t)")

            def mkset(q):
                base = scr if q == 0 else scr2
                nm = ("R0", "R1", "R2") if q == 0 else ("Q0", "Q1", "Q2")
                A0_, S1_, S2_ = base[:, 0:1408], base[:, 1408:2816], base[:, 2816:4224]
                sv3 = lambda t_: t_[:, 1040:1040 + 368].rearrange("p (s r) -> p s r", s=16)
                return dict(A0=A0_, S1=S1_, S2=S2_, A0s=sv3(A0_), S1s=sv3(S1_), S2s=sv3(S2_), nm=nm,
                            DT=DTf_all[:, q * 2048:q * 2048 + 1152], dtr=("sq", q * 4), dtr_all=[("sq", q * 4 + z_) for z_ in range(3)],
                            UGs=A0_[:, 1040:1040 + 160].rearrange("p (s r) -> p s r", s=16),
                            Ys=S1_[:, 1024:1152].rearrange("p (s t) -> p s t", s=16))
            sets = [mkset(0), mkset(1)]

            def poolA(g, S):
                A0, A0s, r0 = S["A0"], S["A0s"], S["nm"][0]
                if pss == 0:
                    P.add("dve", (lambda e: e.tensor_copy(out=A0[:, 0:15], in_=zero15[:, 0:15])), reads=["c_zero"], writes=[r0])
                else:
                    P.add("dve", (lambda e: e.tensor_copy(out=A0[:, 0:15], in_=ahist[:, g, 0:15])), reads=["ahist"], writes=[r0])
                if has_s:
                    P.add("act", (lambda e: e.activation(out=A0s[:, :, 0:15], in_=poolhT[:, g, :].rearrange("p (s r) -> p s r", s=16),
                                                         func=AF.Copy)), reads=["poolhT"], writes=[r0])
                for (si, c0, n) in subs_a:
                    b = nb()
                    mm_group(ps[b][:, 0:n], [(sva[:, k, g * 128:(g + 1) * 128], hT[:, k, c0:c0 + n]) for k in range(8)],
                             [("w", ablk % 4, 0), ("hT", si)], b)
                    if si < 2:
                        P.add("act", (lambda e, b=b, c0=c0, n=n: e.activation(out=A0[:, 15 + c0:15 + c0 + n], in_=ps[b][:, 0:n], func=AF.Copy)),
                              reads=[BK(b)], writes=[r0])
                    else:
                        P.add("act", (lambda e, b=b: e.activation(out=A0s[:, :, 15:23], in_=ps[b][:, 0:128].rearrange("p (s t) -> p s t", s=16),
                                                                  func=AF.Copy)), reads=[BK(b)], writes=[r0])
                P.add("act", (lambda e: e.activation(out=ahist[:, g, 0:15], in_=A0[:, 1024:1039], func=AF.Copy)), reads=[r0], writes=["ahist"])
                if has_s:
                    P.add("act", (lambda e: e.activation(out=npsT[:, g, :].rearrange("p (s r) -> p s r", s=16), in_=A0s[:, :, 8:23],
                                                         func=AF.Copy)), reads=[r0], writes=["npsT"])

            def poolB(g, S):
                w = 2 << g
                A0, A0s = S["A0"], S["A0s"]
                bufs = [(S["A0"], S["A0s"], S["nm"][0]), (S["S1"], S["S1s"], S["nm"][1]), (S["S2"], S["S2s"], S["nm"][2])]
                cur = 0
                kstep = 1
                nxt_order = [1, 2, 1, 2]
                for lvl in range(g + 1):
                    nxt = nxt_order[lvl]
                    srcb, srcs, srcr = bufs[cur]
                    dstb, dsts, dstr = bufs[nxt]
                    lo = 2 * kstep - 1
                    P.add("dve", (lambda e, srcb=srcb, dstb=dstb, lo=lo, kstep=kstep: e.tensor_tensor(
                        out=dstb[:, lo:1039], in0=srcb[:, lo:1039], in1=srcb[:, lo - kstep:1039 - kstep], op=ALU.add)),
                        reads=[srcr], writes=[dstr])
                    if has_s:
                        P.add("dve", (lambda e, srcs=srcs, dsts=dsts, lo=lo, kstep=kstep: e.tensor_tensor(
                            out=dsts[:, :, lo:23], in0=srcs[:, :, lo:23], in1=srcs[:, :, lo - kstep:23 - kstep], op=ALU.add)),
                            reads=[srcr], writes=[dstr])
                    cur = nxt
                    kstep *= 2
                sb_, ss_, sr_ = bufs[cur]
                DT, dtr, r0 = S["DT"], S["dtr"], S["nm"][0]
                dtw = S["dtr_all"]
                P.add("dve", (lambda e: e.scalar_tensor_tensor(
                    out=DT[:, 0:1024], in0=sb_[:, 15:1039], scalar=1.0 / w, in1=A0[:, 15:1039], op0=ALU.mult, op1=ALU.subtract)),
                    reads=[sr_, r0], writes=dtw)
                if pss == 0:
                    P.add("dve", (lambda e: e.tensor_tensor(out=tmp16[:, :], in0=sb_[:, 15:31], in1=invc[:, g, :], op=ALU.mult)),
                          reads=[sr_, "c_invc"], writes=["tmp16"])
                    P.add("dve", (lambda e: e.tensor_tensor(out=DT[:, 0:16], in0=tmp16[:, :], in1=A0[:, 15:31], op=ALU.subtract)),
                          reads=["tmp16", r0], writes=dtw)
                if has_s:
                    P.add("dve", (lambda e: e.scalar_tensor_tensor(
                        out=DT[:, 1024:1152].rearrange("p (s t) -> p s t", s=16), in0=ss_[:, :, 15:23], scalar=1.0 / w, in1=A0s[:, :, 15:23],
                        op0=ALU.mult, op1=ALU.subtract)), reads=[sr_, r0], writes=dtw)

            def poolC(g, S):
                DT, dtr = S["DT"], S["dtr"]
                for (si, c0, n) in subs_a:
                    b = nb()
                    mm_group(ps[b][:, 0:n], [(poolw[:, g, :], DT[:, c0:c0 + n])], S["dtr_all"] + ["c_poolw"], b)
                    P.add("act", (lambda e, b=b, c0=c0, n=n: e.activation(out=U[:, g, c0:c0 + n], in_=ps[b][:, 0:n], func=AF.Copy,
                                                                          scale=psc[:, g:g + 1])),
                          reads=[BK(b), "c_psc"], writes=[("U", si)])

            def convA(j, S):
                UG, UGs, r0 = S["A0"], S["UGs"], S["nm"][0]
                b_cb, b_cc, b_ch = heads
                if j == 0:
                    prefetch(b_ch + 1)
                sv_cc = slot_view_blk(b_cc, 8, 512)
                sv_ch = slot_view_blk(b_ch, 8, 512)
                if pss == 0:
                    P.add("dve", (lambda e: e.tensor_copy(out=UG[:, 0:2], in_=zero15[:, 0:2])), reads=["c_zero"], writes=[r0])
                else:
                    P.add("dve", (lambda e: e.tensor_copy(out=UG[:, 0:2], in_=uhist[:, j, :])), reads=["uhist"], writes=[r0])
                if has_s:
                    P.add("act", (lambda e: e.activation(out=UGs[:, :, 0:2], in_=convhT[:, j, :].rearrange("p (s r) -> p s r", s=16),
                                                         func=AF.Copy)), reads=["convhT"], writes=[r0])
                for (si, c0, n) in subs_a:
                    bc = nb(); bh = nb()
                    mm_group(ps[bc][:, 0:n], [(sv_cc[:, k, j * 128:(j + 1) * 128], hT[:, k, c0:c0 + n]) for k in range(8)], [("w", b_cc % 4, 0), ("hT", si)], bc)
                    mm_group(ps[bh][:, 0:n], [(sv_ch[:, k, j * 128:(j + 1) * 128], hT[:, k, c0:c0 + n]) for k in range(8)], [("w", b_ch % 4, 0), ("hT", si)], bh)
                    sgi = sg_ctr[0] % 2; sg_ctr[0] += 1
                    s_ = sg[sgi]
                    P.add("act", (lambda e, bc=bc, n=n, s_=s_: e.activation(out=s_[:, 0:n], in_=ps[bc][:, 0:n], func=AF.Copy)),
                          reads=[BK(bc)], writes=[("sg", sgi)])
                    if si < 2:
                        P.add("dve", (lambda e, bh=bh, n=n, s_=s_, c0=c0: e.tensor_tensor(out=UG[:, 2 + c0:2 + c0 + n], in0=ps[bh][:, 0:n], in1=s_[:, 0:n],
                                                                                          op=ALU.mult)), reads=[BK(bh), ("sg", sgi)], writes=[r0])
                    else:
                        P.add("dve", (lambda e, bh=bh, s_=s_: e.tensor_tensor(out=UGs[:, :, 2:10], in0=ps[bh][:, 0:128].rearrange("p (s t) -> p s t", s=16),
                                                                              in1=s_[:, 0:128].rearrange("p (s t) -> p s t", s=16), op=ALU.mult)),
                              reads=[BK(bh), ("sg", sgi)], writes=[r0])
                P.add("act", (lambda e: e.activation(out=uhist[:, j, :], in_=UG[:, 1024:1026], func=AF.Copy)), reads=[r0], writes=["uhist"])
                if has_s:
                    P.add("act", (lambda e: e.activation(out=ncsT[:, j, :].rearrange("p (s r) -> p s r", s=16), in_=UGs[:, :, 8:10], func=AF.Copy)),
                          reads=[r0], writes=["ncsT"])

            def convB(j, S):
                UG, UGs, Y, Ys, r0, r1 = S["A0"], S["UGs"], S["S1"], S["Ys"], S["nm"][0], S["nm"][1]
                segs = [(Y[:, 0:1024], lambda d: UG[:, 2 - d:1026 - d])]
                if has_s:
                    segs.append((Ys, lambda d: UGs[:, :, 2 - d:10 - d]))
                for (yv, uf) in segs:
                    P.add("dve", (lambda e, yv=yv, uf=uf: e.tensor_scalar(out=yv, in0=uf(2), scalar1=cw[:, j:j + 1], scalar2=None, op0=ALU.mult)),
                          reads=[r0, "c_cw"], writes=[r1])
                    for (kk_, d) in ((1, 1), (2, 0)):
                        P.add("dve", (lambda e, yv=yv, uf=uf, kk_=kk_, d=d: e.scalar_tensor_tensor(
                            out=yv, in0=uf(d), scalar=cw[:, kk_ * 4 + j:kk_ * 4 + j + 1], in1=yv, op0=ALU.mult, op1=ALU.add)),
                            reads=[r0, r1, "c_cw"], writes=[r1])

            def convC(j, S):
                Y, r1 = S["S1"], S["nm"][1]
                b_cb = heads[0]
                sv_cb = slot_view_blk(b_cb, 8, 512)
                for (si, c0, n) in subs_a:
                    b = nb()
                    mm_group(ps[b][:, 0:n], [(sv_cb[:, k, j * 128:(j + 1) * 128], hT[:, k, c0:c0 + n]) for k in range(8)], [("w", b_cb % 4, 0), ("hT", si)], b)
                    P.add("dve", (lambda e, b=b, c0=c0, n=n: e.tensor_tensor(out=U[:, 4 + j, c0:c0 + n], in0=ps[b][:, 0:n], in1=Y[:, c0:c0 + n], op=ALU.mult)),
                          reads=[BK(b), r1], writes=[("U", si)])

            steps = [(poolA, poolB, poolC, g) for g in range(4)] + [(convA, convB, convC, j) for j in range(4)]
            for t_ in range(len(steps) + 1):
                if t_ >= 1:
                    _, fb, _, idx = steps[t_ - 1]
                    fb(idx, sets[(t_ - 1) % 2])
                if t_ < len(steps):
                    fa, _, _, idx = steps[t_]
                    fa(idx, sets[t_ % 2])
                if t_ >= 1:
                    _, _, fc, idx = steps[t_ - 1]
                    fc(idx, sets[(t_ - 1) % 2])

            if pss == 1:
                b = nb()
                for g in range(4):
                    transp(ps[b][0:15, g * 128:(g + 1) * 128], ahist[:, g, 0:15], identf[:], ["ahist"], b)
                i = stg_ctr[0] % 2
                stg_ctr[0] += 1
                P.add("act", (lambda e, b=b, i=i: e.activation(out=stg[i][0:15, :], in_=ps[b][0:15, :], func=AF.Copy)), reads=[BK(b)], writes=[("stg", i)])
                P.add("sp", (lambda e, i=i: e.dma_start(out=np_p[:, :], in_=stg[i][0:15, :])), reads=[("stg", i)], dma_key=f"stg{i}")
                b = nb()
                for j in range(4):
                    transp(ps[b][0:2, j * 128:(j + 1) * 128], uhist[:, j, :], identf[:], ["uhist"], b)
                i = stg_ctr[0] % 2
                stg_ctr[0] += 1
                P.add("act", (lambda e, b=b, i=i: e.activation(out=stg[i][0:2, :], in_=ps[b][0:2, :], func=AF.Copy)), reads=[BK(b)], writes=[("stg", i)])
                P.add("sp", (lambda e, i=i: e.dma_start(out=nc_p[:, :], in_=stg[i][0:2, :])), reads=[("stg", i)], dma_key=f"stg{i}")
            if has_s:
                for half in range(2):
                    b = nb()
                    for g in range(4):
                        transp(ps[b][0:120, g * 128:(g + 1) * 128], npsT[:, g, half * 120:(half + 1) * 120], identf[:], ["npsT"], b)
                    i = stg_ctr[0] % 2
                    stg_ctr[0] += 1
                    P.add("act", (lambda e, b=b, i=i: e.activation(out=stg[i][0:120, :], in_=ps[b][0:120, :], func=AF.Copy)), reads=[BK(b)], writes=[("stg", i)])
                    P.add("sp", (lambda e, i=i, half=half: e.dma_start(out=np_s[half * 120:(half + 1) * 120, :], in_=stg[i][0:120, :])),
                          reads=[("stg", i)], dma_key=f"stg{i}")
                b = nb()
                for j in range(4):
                    transp(ps[b][0:32, j * 128:(j + 1) * 128], ncsT[:, j, :], identf[:], ["ncsT"], b)
                i = stg_ctr[0] % 2
                stg_ctr[0] += 1
                P.add("act", (lambda e, b=b, i=i: e.activation(out=stg[i][0:32, :], in_=ps[b][0:32, :], func=AF.Copy)), reads=[BK(b)], writes=[("stg", i)])
                P.add("sp", (lambda e, i=i: e.dma_start(out=nc_s[:, :], in_=stg[i][0:32, :])), reads=[("stg", i)], dma_key=f"stg{i}")
            mo = sched[("mo", pss)]
            prefetch(mo[1] + 2)
            proj_accum(mo, subs_a, (lambda si, c0, n, k: U[:, k, c0:c0 + n]), (lambda si: [("U", si)]), False,
                       after=(next_norm(2) if "attn" not in skip else None), after_early=True)
        dbg_dump(("mix", pss))

        if pss == 0 and "sattn" not in skip:
            sa_load_k(0)
        if "attn" not in skip:
            if "mix" in skip:
                norm(xT, subs_a, 2, hT)
            xq = sched[("xq", pss)]
            prefetch(xq[1] + 2)
            for (si, c0, n) in subs_a:
                for o in range(8):
                    blk = xq[o // 4]
                    sv = slot_view_blk(blk, 8, 512)
                    b = nb()
                    mm_group(ps[b][:, 0:n], [(sv[:, k, (o % 4) * 128:(o % 4 + 1) * 128], hT[:, k, c0:c0 + n]) for k in range(8)],
                             wres(blk, 1) + [("hT", si)], b)
                    if si < 2:
                        P.add("act", (lambda e, b=b, o=o, c0=c0, n=n: e.activation(out=U[:, o, c0:c0 + n], in_=ps[b][:, 0:n], func=AF.Copy)),
                              reads=[BK(b)], writes=[("U", si)])
                    else:
                        P.add("act", (lambda e, b=b, o=o: e.activation(out=qTs[:, o, :], in_=ps[b][:, 0:128], func=AF.Copy)),
                              reads=[BK(b)], writes=["qTs"])
            xo = sched[("xo", pss)]
            prefetch(xo[1] + 2)
            items = [(si, c0, n, h) for (si, c0, n) in SUB_P for h in range(4)]
            st_ = {}

            def at_s1(i):
                (si, c0, n, h) = items[i]
                bs = [nb(), nb()]
                for mc in range(2):
                    mm_group(ps[bs[mc]][:, 0:n],
                             [(kTp[:, h * 2 + dc, mc * 128:(mc + 1) * 128], U[:, h * 2 + dc, c0:c0 + n]) for dc in range(2)],
                             ["kTp", ("U", si)], bs[mc])
                ei = i % 2
                E = ET[ei]
                for mc in range(2):
                    P.add("act", (lambda e, mc=mc, bb=bs[mc]: e.activation(out=E[:, mc, 0:n], in_=ps[bb][:, 0:n], func=AF.Exp, scale=1.0 / 16.0)),
                          reads=[BK(bs[mc])], writes=[("ET", ei)])

            def at_s2(i):
                (si, c0, n, h) = items[i]
                ei = i % 2
                E = ET[ei]
                bd = nb()
                mm_group(ps[bd][:, 0:n], [(ones1[:], E[:, mc, 0:n]) for mc in range(2)], [("ET", ei), "c_ones1"], bd)
                r = rstd[ei]
                P.add("act", (lambda e: e.activation(out=r[:, 0:n], in_=ps[bd][:, 0:n], func=AF.Ln)), reads=[BK(bd)], writes=[("rstd", ei)])
                P.add("act", (lambda e: e.activation(out=r[:, 0:n], in_=r[:, 0:n], func=AF.Exp, scale=-1.0)), reads=[("rstd", ei)], writes=[("rstd", ei)])
                for dc in range(2):
                    bo = nb()
                    c = h * 2 + dc
                    mm_group(ps[bo][:, 0:n], [(Vp[:, mc, c * 128:(c + 1) * 128], E[:, mc, 0:n]) for mc in range(2)], ["Vp", ("ET", ei)], bo)
                    P.add("dve", (lambda e, bo=bo, c=c: e.tensor_tensor(out=hT[:, c, c0:c0 + n], in0=ps[bo][:, 0:n], in1=r[:, 0:n], op=ALU.mult)),
                          reads=[BK(bo), ("rstd", ei)], writes=[("hT", si)])

            at_s1(0)
            for i in range(len(items)):
                if i + 1 < len(items):
                    at_s1(i + 1)
                at_s2(i)

            def xo_rhs(si, c0, n, k):
                return hT[:, k, c0:c0 + n] if si < 2 else oTs[:, k, :]

            proj_accum(xo, subs_b, xo_rhs, (lambda si: [("hT", si)] if si < 2 else ["oTs"]), False,
                       after=(next_norm(4) if "ffn2" not in skip else None), after_early=True)
        dbg_dump(("attn", pss))

        hooks = None
        if pss == 0 and "sattn" not in skip:
            hooks = [(lambda t=t: sa_hook(t)) for t in range(18)]
        if "ffn2" not in skip:
            aft = final_sub if "final" not in skip else None
            if pss == 0 and preload_next:
                def aft(sub, bss=None):
                    (si_, c0_, n_) = sub
                    if si_ == 0:
                        stage_loads(1)
                    final_norm(sub, si_, bss=bss)
                    if si_ == 1:
                        while deferred_out:
                            deferred_out.pop(0)()
                    staged_to_xT(si_)
                    next_norm(0)(sub)
                    for tcn_ in range(4):
                        deferred_out.append((lambda sub=sub, tcn_=tcn_, w_=si_: final_out_chunk(sub, tcn_, 0, w_)))
            ffn(("f2", pss), 4, subs_b, hooks=hooks, prenormed=("attn" not in skip), after=aft, after_early=(pss == 1),
                pre_last=((lambda: stage_loads(0)) if (pss == 0 and preload_next) else None))
        elif hooks is not None:
            for hk in hooks:
                hk()
        dbg_dump(("ffn2", pss))

    assert skip or wb_emitted[0] == len(wblocks), (wb_emitted[0], len(wblocks))
    P.finalize()

    dkeys = sorted(P.dma_count.keys())
    esem = {e: es.enter_context(nc.semaphore(f"s_{e}")) for e in ENGS}
    dsem = {k: es.enter_context(nc.semaphore(f"d_{k}")) for k in dkeys}
    with nc.Block() as block:
        @block.tensor
        def _(t):
            P.emit("pe", t, esem, dsem)

        @block.scalar
        def _(a):
            P.emit("act", a, esem, dsem)

        @block.vector
        def _(v):
            P.emit("dve", v, esem, dsem)

        @block.gpsimd
        def _(g):
            P.emit("pool", g, esem, dsem)

        @block.sync
        def _(s):
            P.emit("sp", s, esem, dsem, final_wait=True)
    es.close()
    return nc


_CACHE = {}


def _consts():
    ident = np.eye(128, dtype=np.float32)
    invc = np.zeros((128, 4, 16), np.float32)
    for g in range(4):
        w = 2 << g
        for t in range(16):
            invc[:, g, t] = 1.0 / min(t + 1, w)
    return ident, invc.reshape(128, 64)


def make_in_maps(inp):
    f = lambda a: np.ascontiguousarray(np.asarray(a, dtype=np.float32))
    ident, invc = _consts()
    gains = f(np.stack([inp["g_ffn1"][0], inp["g_mix"][0], inp["g_xq"][0], inp["g_mem"][0], inp["g_ffn2"][0], inp["g_final"]]))
    gains = f(gains.reshape(6, 8, 128).transpose(2, 0, 1).reshape(128, 48))
    shared = {
        "w_f1i": f(inp["w_ffn1_in"][0]), "w_f1o": f(inp["w_ffn1_out"][0]),
        "w_f2i": f(inp["w_ffn2_in"][0]), "w_f2o": f(inp["w_ffn2_out"][0]),
        "w_mi": f(inp["w_mix_in"][0]), "w_mo": f(inp["w_mix_out"][0]),
        "w_xq": f(inp["w_xq"][0]), "w_xk": f(inp["w_xk"][0]), "w_xv": f(inp["w_xv"][0]), "w_xo": f(inp["w_xo"][0]),
        "pool_w": f(inp["pool_w"][0]).reshape(512, 128),
        "gains": gains,
        "pool_scale": f(f(inp["pool_scale"][0]).reshape(4, 128).T),
        "conv_w": f(f(inp["conv_w"][0]).reshape(12, 128).T),
        "c_ident": ident, "c_invc": invc,
    }
    maps = []
    for c in range(NCORES):
        s0, s1 = 16 * c, 16 * (c + 1)
        m = dict(shared)
        m["x_p"] = f(inp["x_prompt"][c])
        m["x_s"] = f(inp["x_sample"][s0:s1]).reshape(128, D)
        m["mem"] = f(inp["mem_prompt"][c])
        m["st_pool"] = f(inp["state_pool"][0, s0:s1]).reshape(240, 512)
        m["st_conv"] = f(inp["state_conv"][0, s0:s1]).reshape(32, 512)
        m["ck"] = f(inp["cache_mem_k"][0, s0:s1]).reshape(4096, D)
        m["cv"] = f(inp["cache_mem_v"][0, s0:s1]).reshape(4096, D)
        maps.append(m)
    return maps


def kernel(**inputs):
    if "nc" not in _CACHE:
        _CACHE["nc"] = build_program()
    nc = _CACHE["nc"]
    maps = make_in_maps(inputs)
    res = run_bass_kernel_spmd(nc, maps, core_ids=list(range(NCORES)))
    R = res.results
    y_prompt = np.stack([R[c]["y_p"] for c in range(NCORES)]).astype(np.float32)
    y_sample = np.concatenate([R[c]["y_s"].reshape(16, 8, D) for c in range(NCORES)]).astype(np.float32)
    npp = np.stack([R[c]["np_p"] for c in range(NCORES)])[None].astype(np.float32)
    ncp = np.stack([R[c]["nc_p"] for c in range(NCORES)])[None].astype(np.float32)
    nkp = np.stack([R[c]["nk_p"].reshape(256, 4, 256) for c in range(NCORES)])[None].astype(np.float32)
    nvp = np.stack([R[c]["nv_p"].reshape(256, 4, 256) for c in range(NCORES)])[None].astype(np.float32)
    nps = np.concatenate([R[c]["np_s"].reshape(16, 15, 512) for c in range(NCORES)])[None].astype(np.float32)
    ncs = np.concatenate([R[c]["nc_s"].reshape(16, 2, 512) for c in range(NCORES)])[None].astype(np.float32)
    return (y_prompt, y_sample, npp, ncp, nkp, nvp, nps, ncs)
```

```python
import numpy as np
from contextlib import ExitStack
import concourse.bass as bass
import concourse.mybir as mybir
from concourse.bass_utils import run_bass_kernel_spmd

F32 = mybir.dt.float32
BF16 = mybir.dt.bfloat16
AF = mybir.ActivationFunctionType
ALU = mybir.AluOpType

NCORES = 8
D = 1024
DFF = 2816
NTM = 1152
EPS = 1e-6
ENGS = ("pe", "act", "dve", "pool", "sp")
SAME_ENG_WAR = True
SAME_ENG_WAW = False
INC_NORM = True
WAIT_COUNT = {}


class _Op:
    __slots__ = ("eng", "fn", "deps", "signal", "dma_key", "dma_idx", "sig_val", "group")

    def __init__(self, eng, fn, dma_key):
        self.group = None
        self.eng = eng
        self.fn = fn
        self.deps = []
        self.signal = False
        self.dma_key = dma_key
        self.dma_idx = 0
        self.sig_val = 0


class Prog:
    def __init__(self):
        self.ops = {e: [] for e in ENGS}
        self.last_w = {}
        self.readers = {}
        self.dma_count = {}
        self.wait_all_keys = set()

    def add(self, eng, fn, reads=(), writes=(), dma_key=None, group=None):
        o = _Op(eng, fn, dma_key)
        o.group = group
        is_dma = dma_key is not None
        deps = {}
        for r in reads:
            w = self.last_w.get(r)
            if w is not None:
                deps[id(w)] = w
        for w_ in writes:
            lw = self.last_w.get(w_)
            if lw is not None and (SAME_ENG_WAW or is_dma or lw.dma_key is not None or lw.eng != eng):
                deps[id(lw)] = lw
            for rd in self.readers.get(w_, {}).values():
                if SAME_ENG_WAR or is_dma or rd.dma_key is not None or rd.eng != eng:
                    deps[id(rd)] = rd
        for d in deps.values():
            if d is o or (group is not None and d.group == group):
                continue
            if d.dma_key is None and not is_dma and d.eng == "pe" and eng == "pe":
                continue
            o.deps.append(d)
            d.signal = True
        stream = ("dma", dma_key) if is_dma else eng
        for r in reads:
            self.readers.setdefault(r, {})[stream] = o
        for w_ in writes:
            self.last_w[w_] = o
            self.readers[w_] = {}
        if is_dma:
            self.dma_count[dma_key] = self.dma_count.get(dma_key, 0) + 1
            o.dma_idx = self.dma_count[dma_key]
        self.ops[eng].append(o)
        return o

    def finalize(self):
        for e in ENGS:
            c = 0
            for o in self.ops[e]:
                if o.dma_key is None and o.signal:
                    c += 1
                    o.sig_val = c

    def emit(self, eng, handle, esem, dsem, final_wait=False):
        waited = {}
        for o in self.ops[eng]:
            for d in o.deps:
                if d.dma_key is not None:
                    sem = dsem[d.dma_key]
                    if d.dma_key in self.wait_all_keys:
                        val = 16 * self.dma_count[d.dma_key]
                    else:
                        val = 16 * d.dma_idx
                else:
                    sem = esem[d.eng]
                    val = d.sig_val
                key = id(sem)
                if waited.get(key, 0) < val:
                    handle.wait_ge(sem, val)
                    waited[key] = val
                    WAIT_COUNT[eng] = WAIT_COUNT.get(eng, 0) + 1
            ins = o.fn(handle)
            if o.dma_key is not None:
                ins.then_inc(dsem[o.dma_key], 16)
            elif o.signal:
                ins.then_inc(esem[eng], 1)
        if final_wait:
            for k, cnt in self.dma_count.items():
                handle.wait_ge(dsem[k], 16 * cnt)


def build_program(dbg_stage=None, skip=()):
    nc = bass.Bass("TRN2", target_bir_lowering=False)
    P = Prog()

    def din(name, shape, dt=F32):
        return nc.dram_tensor(name, list(shape), dt, kind="ExternalInput").ap()

    def dout(name, shape, dt=F32):
        return nc.dram_tensor(name, list(shape), dt, kind="ExternalOutput").ap()

    x_p = din("x_p", [2048, D]); x_s = din("x_s", [128, D]); mem = din("mem", [256, D])
    st_pool = din("st_pool", [240, 512]); st_conv = din("st_conv", [32, 512])
    ck = din("ck", [4096, D]); cv = din("cv", [4096, D])
    w_f1i = din("w_f1i", [D, 2 * DFF]); w_f1o = din("w_f1o", [DFF, D])
    w_f2i = din("w_f2i", [D, 2 * DFF]); w_f2o = din("w_f2o", [DFF, D])
    w_mi = din("w_mi", [D, 2048]); w_mo = din("w_mo", [D, D])
    w_xq = din("w_xq", [D, D]); w_xk = din("w_xk", [D, D]); w_xv = din("w_xv", [D, D]); w_xo = din("w_xo", [D, D])
    pool_w = din("pool_w", [512, 128])
    gains_d = din("gains", [128, 48])
    psc_d = din("pool_scale", [128, 4]); cw_d = din("conv_w", [128, 12])
    c_ident = din("c_ident", [128, 128]); c_invc = din("c_invc", [128, 64])

    y_p = dout("y_p", [2048, D]); y_s = dout("y_s", [128, D])
    np_p = dout("np_p", [15, 512]); nc_p = dout("nc_p", [2, 512])
    nk_p = dout("nk_p", [256, D]); nv_p = dout("nv_p", [256, D])
    np_s = dout("np_s", [240, 512]); nc_s = dout("nc_s", [32, 512])
    dbg = dout("dbg", [128, 8 * NTM]) if dbg_stage is not None else None

    es = ExitStack()

    def sb(name, shape, dt):
        return es.enter_context(nc.sbuf_tensor(name, list(shape), dt))

    xT = sb("xT", [128, 8, NTM], F32)
    hT = sb("hT", [128, 8, NTM], BF16)
    U = sb("U", [128, 8, NTM], BF16)
    wsl = [sb(f"wsl{i}", [128, 4096], BF16) for i in range(4)]
    sq = sb("sq", [128, 8, 512], BF16)
    rstd = [sb(f"rstd{i}", [128, 512], F32) for i in range(2)]
    sg = [sb(f"sg{i}", [128, 512], F32) for i in range(2)]
    NIO = 4
    io = [sb(f"io{i}", [128, 1024], F32) for i in range(NIO)]
    scr = sb("scr", [128, 3 * 1408], F32)
    scr2 = sb("scr2", [128, 3 * 1408], F32)
    scr_bf = scr.bitcast(BF16)
    poolhT = sb("poolhT", [128, 4, 240], F32)
    npsT = sb("npsT", [128, 4, 240], F32)
    convhT = sb("convhT", [128, 4, 32], F32)
    ncsT = sb("ncsT", [128, 4, 32], F32)
    kTp = sb("kTp", [128, 8, 256], BF16)
    Vp = sb("Vp", [128, 2, 1024], BF16)
    qTs = sb("qTs", [128, 8, 128], BF16)
    oTs = sb("oTs", [128, 8, 128], BF16)
    ET = [sb(f"ET{i}", [128, 2, 512], BF16) for i in range(2)]
    gains = sb("gains_sb", [128, 6, 8], F32)
    psc = sb("psc", [128, 4], F32)
    cw = sb("cw", [128, 12], F32)
    ones_n = sb("ones_n", [128, 128], BF16)
    ones1 = sb("ones1", [128, 128], BF16)
    identf = sb("identf", [128, 128], F32)
    identb = sb("identb", [128, 128], BF16)
    poolw = sb("poolw", [128, 4, 128], BF16)
    ahist = sb("ahist", [128, 4, 16], F32)
    uhist = sb("uhist", [128, 4, 2], F32)
    invc = sb("invc", [128, 4, 16], F32)
    ETs = sb("ETs", [128, 64], BF16)
    ETsB = sb("ETsB", [128, 64], BF16)
    rdens = sb("rdens", [128, 32], F32)
    tmp16 = sb("tmp16", [128, 16], F32)
    stg = [sb(f"stg{i}", [128, 512], F32) for i in range(2)]
    zero15 = sb("zero15", [128, 16], F32)

    ps = [es.enter_context(nc.psum_tensor(f"ps{i}", [128, 512], F32)) for i in range(8)]
    ps_bf = [p_.bitcast(BF16) for p_ in ps]

    A0 = scr[:, 0:1408]; S1 = scr[:, 1408:2816]; S2 = scr[:, 2816:4224]
    Knat = scr_bf[:, 0:2048]; Vnat = scr_bf[:, 2816:2816 + 2048]; kTs = scr_bf[:, 5632:5632 + 2048]
    yT = scr[:, 0:4096]

    bank_ctr = [0]
    sg_ctr = [0]

    pinned = set()

    def nb():
        while True:
            b = bank_ctr[0] % 8
            bank_ctr[0] += 1
            if b not in pinned:
                return b

    def BK(b):
        return ("ps", b)

    def mm_group(out_ap, pairs, reads, b):
        n = len(pairs)
        for i, (l, r) in enumerate(pairs):
            P.add("pe", (lambda e, l=l, r=r, i=i: e.matmul(out_ap, lhsT=l, rhs=r, start=(i == 0), stop=(i == n - 1))),
                  reads=reads, writes=[BK(b)])

    def transp(out_ap, in_ap, ident_ap, reads, b):
        P.add("pe", (lambda e: e.transpose(out_ap, in_ap, ident_ap)), reads=reads + ["c_identf", "c_identb"], writes=[BK(b)])

    P.wait_all_keys.add("init")

    def init_load(out_ap, in_ap, res, eng="sp"):
        P.add(eng, (lambda e: e.dma_start(out=out_ap, in_=in_ap)), writes=[res], dma_key=("init" if eng == "sp" else "initp"))

    init_load(identf[:], c_ident[:, :], "c_identf")
    init_load(invc[:], c_invc.rearrange("p (g c) -> p g c", g=4), "c_invc")
    init_load(gains[:], gains_d.rearrange("p (g k) -> p g k", g=6), "c_gains")
    init_load(psc[:], psc_d[:, :], "c_psc")
    init_load(cw[:], cw_d[:, :], "c_cw")
    init_load(poolw[:], pool_w.rearrange("(g c) d -> c g d", g=4), "c_poolw", eng="pool")
    P.add("dve", lambda e: e.memset(ones_n[:], 1.0 / 1024.0), writes=["c_ones_n"])
    P.add("dve", lambda e: e.memset(ones1[:], 1.0), writes=["c_ones1"])
    P.add("dve", lambda e: e.memset(zero15[:], 0.0), writes=["c_zero"])
    P.add("dve", lambda e: e.tensor_copy(out=identb[:], in_=identf[:]), reads=["c_identf"], writes=["c_identb"])

    wblocks = []
    wb_emitted = [0]

    def wblock(parts):
        wblocks.append(parts)
        return len(wblocks) - 1

    def slot_view(slot, kk, width):
        return wsl[slot][:, 0:kk * width].rearrange("p (k n) -> p k n", k=kk)

    slot_content = {}

    def slot_view_blk(blk, kk, width):
        assert slot_content.get(blk % 4) == blk, ("weight slot no longer holds block", blk, slot_content)
        return slot_view(blk % 4, kk, width)

    def prefetch(upto):
        upto = min(upto, len(wblocks) - 1)
        while wb_emitted[0] <= upto:
            b = wb_emitted[0]
            slot = b % 4
            slot_content[slot] = b
            batch = []
            for pi, (kk, width, c0, n, src) in enumerate(wblocks[b]):
                dst = slot_view(slot, kk, width)[:, :, c0:c0 + n]
                batch.append(P.add("pool", (lambda e, dst=dst, src=src: e.dma_start(out=dst, in_=src.rearrange("(k p) n -> p k n", p=128))),
                                   writes=[("w", slot, q) for q in range(3)], dma_key=f"w{slot}", group=("wb", b)))
            for o_ in batch:
                o_.dma_idx = batch[-1].dma_idx
            wb_emitted[0] += 1

    def wres(b, npart):
        return [("w", b % 4, pi) for pi in range(npart)]

    sched = {}

    def ffn_blocks(tag, w_in, w_out, inserts=None):
        thirds = [(0, 8), (8, 8), (16, 6)]
        out = []
        n_in = 0
        for (j0, kh) in thirds:
            ins = []
            for jj in range(0, kh, 2):
                j = j0 + jj
                ins.append(wblock([(8, 512, 0, 256, w_in[:, j * 128:(j + 2) * 128]),
                                   (8, 512, 256, 256, w_in[:, DFF + j * 128:DFF + (j + 2) * 128])]))
                if inserts is not None and n_in in inserts:
                    sq_blocks(*inserts[n_in])
                n_in += 1
            outs = [wblock([(kh, 512, 0, 512, w_out[j0 * 128:(j0 + kh) * 128, cb * 512:(cb + 1) * 512])]) for cb in range(2)]
            out.append((j0, kh, ins, outs))
        sched[tag] = out

    def sq_blocks(tag, w):
        sched[tag] = [wblock([(8, 512, 0, 512, w[:, cb * 512:(cb + 1) * 512])]) for cb in range(2)]

    def mix_blocks(tag):
        blks = [wblock([(8, 512, 0, 512, w_mi[:, 512 * i:512 * (i + 1)])]) for i in range(4)]
        sched[tag] = (blks[0], blks[1:])

    for pss in range(2):
        ffn_blocks(("f1", pss), w_f1i, w_f1o, inserts=({1: ("xk", w_xk), 2: ("xv", w_xv)} if pss == 0 else None))
        mix_blocks(("mi", pss))
        sq_blocks(("mo", pss), w_mo)
        sq_blocks(("xq", pss), w_xq)
        sq_blocks(("xo", pss), w_xo)
        ffn_blocks(("f2", pss), w_f2i, w_f2o)

    def XR(si, k):
        return ("x", si, k)

    rs_ctr = [0]

    SQ_ALL = [("sq", o_) for o_ in range(8)]

    def norm_sub(src, sub, gi, dst, bss=None):
        (si, c0, n) = sub
        xr = [XR(si, k) for k in range(8)]
        if bss is None:
            P.add("act", (lambda e: e.activation(out=sq[:, :, 0:n], in_=src[:, :, c0:c0 + n], func=AF.Square)),
                  reads=xr, writes=SQ_ALL)
            b = nb()
            mm_group(ps[b][:, 0:n], [(ones_n[:], sq[:, k, 0:n]) for k in range(8)], SQ_ALL + ["c_ones_n"], b)
        else:
            b = bss
        ri = rs_ctr[0] % 2
        rs_ctr[0] += 1
        r = rstd[ri]
        rr = ("rstd", ri)
        P.add("act", (lambda e: e.activation(out=r[:, 0:n], in_=ps[b][:, 0:n], func=AF.Ln, bias=EPS, scale=1.0)),
              reads=[BK(b)], writes=[rr])
        P.add("act", (lambda e: e.activation(out=r[:, 0:n], in_=r[:, 0:n], func=AF.Exp, scale=-0.5)), reads=[rr], writes=[rr])
        if bss is not None:
            pinned.discard(bss)
        return r, rr, xr

    def norm_apply(src, sub, gi, dst, r, rr, xr, dst_res):
        (si, c0, n) = sub
        for k in range(8):
            P.add("dve", (lambda e, k=k: e.scalar_tensor_tensor(
                out=dst(k, c0, n), in0=src[:, k, c0:c0 + n], scalar=gains[:, gi, k:k + 1], in1=r[:, 0:n],
                op0=ALU.mult, op1=ALU.mult)),
                reads=[xr[k], rr, "c_gains"], writes=dst_res(si))

    def norm(src, subs, gi, dst_t):
        for sub in subs:
            r, rr, xr = norm_sub(src, sub, gi, dst_t)
            norm_apply(src, sub, gi, (lambda k, c0, n: dst_t[:, k, c0:c0 + n]), r, rr, xr, (lambda si: [("hT", si)]))

    def proj_accum(blocks, subs, rhs_of, rhs_res_of, half_scale, kk=8, after=None, sub_hook=None, after_early=False):
        pending = []
        inc = after is not None and INC_NORM
        for (si, c0, n) in subs:
            bss = None
            if inc:
                bss = nb()
                pinned.add(bss)
            ssq = []

            def ss_mm(o_, bss=bss, n=n):
                P.add("pe", (lambda e: e.matmul(ps[bss][:, 0:n], lhsT=ones_n[:], rhs=sq[:, o_, 0:n], start=(o_ == 0), stop=(o_ == 7))),
                      reads=[("sq", o_), "c_ones_n"], writes=[BK(bss)])
            for o in range(8):
                blk = blocks[o // 4]
                sv = slot_view_blk(blk, kk, 512)
                b = nb()
                mm_group(ps[b][:, 0:n], [(sv[:, k, (o % 4) * 128:(o % 4 + 1) * 128], rhs_of(si, c0, n, k)) for k in range(kk)],
                         wres(blk, 1) + rhs_res_of(si), b)
                if half_scale:
                    P.add("dve", (lambda e, b=b, o=o, c0=c0, n=n: e.scalar_tensor_tensor(
                        out=xT[:, o, c0:c0 + n], in0=ps[b][:, 0:n], scalar=0.5, in1=xT[:, o, c0:c0 + n], op0=ALU.mult, op1=ALU.add)),
                        reads=[BK(b), XR(si, o)], writes=[XR(si, o)])
                else:
                    P.add("dve", (lambda e, b=b, o=o, c0=c0, n=n: e.tensor_tensor(
                        out=xT[:, o, c0:c0 + n], in0=ps[b][:, 0:n], in1=xT[:, o, c0:c0 + n], op=ALU.add)),
                        reads=[BK(b), XR(si, o)], writes=[XR(si, o)])
                if inc:
                    P.add("act", (lambda e, o=o, c0=c0, n=n: e.activation(out=sq[:, o, 0:n], in_=xT[:, o, c0:c0 + n], func=AF.Square)),
                          reads=[XR(si, o)], writes=[("sq", o)])
                    ssq.append(o)
                    if len(ssq) > 4:
                        ss_mm(ssq.pop(0))
                if after is not None and after_early and pending and o == 2:
                    after(*pending.pop())
            while ssq:
                ss_mm(ssq.pop(0))
            if sub_hook is not None:
                sub_hook()
            if after is not None:
                if pending:
                    after(*pending.pop())
                pending.append(((si, c0, n), bss))
        if after is not None and pending:
            after(*pending.pop())

    def ffn(tag, gi, subs, hooks=None, prenormed=False, after=None, pre_last=None, blk_hooks=None, after_early=True):
        if not prenormed:
            norm(xT, subs, gi, hT)
        hook_st = [0]
        blk_i = 0

        def one_hook():
            if hooks is not None and hook_st[0] < len(hooks):
                hooks[hook_st[0]]()
                hook_st[0] += 1

        one_hook()
        def in_piece(blk, bi, jj, sub):
            (si, c0, n) = sub
            sv = slot_view_blk(blk, 8, 512)
            jl = bi * 2 + jj
            bg = nb(); bu = nb()
            mm_group(ps[bg][:, 0:n], [(sv[:, k, jj * 128:(jj + 1) * 128], hT[:, k, c0:c0 + n]) for k in range(8)],
                     [("w", blk % 4, 0), ("hT", si)], bg)
            mm_group(ps[bu][:, 0:n], [(sv[:, k, 256 + jj * 128:256 + (jj + 1) * 128], hT[:, k, c0:c0 + n]) for k in range(8)],
                     [("w", blk % 4, 1), ("hT", si)], bu)
            sgi = sg_ctr[0] % 2; sg_ctr[0] += 1
            s_ = sg[sgi]
            P.add("act", (lambda e: e.activation(out=s_[:, 0:n], in_=ps[bg][:, 0:n], func=AF.Silu)),
                  reads=[BK(bg)], writes=[("sg", sgi)])
            P.add("dve", (lambda e: e.tensor_tensor(out=U[:, jl, c0:c0 + n], in0=ps[bu][:, 0:n], in1=s_[:, 0:n], op=ALU.mult)),
                  reads=[BK(bu), ("sg", sgi)], writes=[("U", si)])

        for (j0, kh, ins, outs) in sched[tag]:
            bi = 0
            while bi < len(ins):
                if j0 == 0 and bi == 0 and len(ins) >= 2:
                    prefetch(ins[1] + 2)
                    for s_i, sub in enumerate(subs):
                        for b2 in range(2):
                            for jj in range(2):
                                in_piece(ins[b2], b2, jj, sub)
                        if s_i < 2:
                            one_hook()
                    blk_i += 2
                    bi += 2
                    if blk_hooks is not None and 1 in blk_hooks:
                        blk_hooks[1]()
                    continue
                blk = ins[bi]
                prefetch(blk + 3)
                for jj in range(2):
                    for sub in subs:
                        in_piece(blk, bi, jj, sub)
                blk_i += 1
                one_hook()
                if blk_hooks is not None and j0 == 0 and bi in blk_hooks:
                    blk_hooks[bi]()
                bi += 1
            prefetch(outs[1] + 2)
            if j0 == 16 and pre_last is not None:
                pre_last()
            proj_accum(outs, subs, (lambda si, c0, n, k: U[:, k, c0:c0 + n]), (lambda si: [("U", si)]), True, kk=kh,
                       after=(after if j0 == 16 else None), sub_hook=one_hook, after_early=after_early)
        assert hooks is None or hook_st[0] == len(hooks), "sample-attention hooks must finish before the epilogue reuses scr"

    io_ctr = [0]

    def load_tokens(src_rows_ap, col0, si, dst3=None, res_of=None):
        dst3 = xT if dst3 is None else dst3
        res_of = (lambda k: XR(si, k)) if res_of is None else res_of
        i = io_ctr[0] % NIO
        io_ctr[0] += 1
        buf = io[i]
        P.add("sp", (lambda e: e.dma_start(out=buf[:], in_=src_rows_ap)), writes=[("io", i)], dma_key=f"io{i}")
        for half in range(2):
            b = nb()
            for j in range(4):
                k = half * 4 + j
                transp(ps[b][:, j * 128:(j + 1) * 128], buf[:, k * 128:(k + 1) * 128], identf[:], [("io", i)], b)
            dstv = dst3[:, half * 4:half * 4 + 4, col0:col0 + 128]
            srcv = ps[b][:, :].rearrange("p (j t) -> p j t", j=4)
            eng = "act" if half == 0 else "dve"
            if eng == "act":
                P.add("act", (lambda e, dstv=dstv, srcv=srcv: e.activation(out=dstv, in_=srcv, func=AF.Copy)),
                      reads=[BK(b)], writes=[res_of(half * 4 + j) for j in range(4)])
            else:
                P.add("dve", (lambda e, dstv=dstv, srcv=srcv: e.tensor_copy(out=dstv, in_=srcv)),
                      reads=[BK(b)], writes=[res_of(half * 4 + j) for j in range(4)])

    hT_f = hT.bitcast(F32)
    U_f = U.bitcast(F32)
    xstage = {0: (hT_f[:, :, :].rearrange("p k t -> p (k t)")[:, 0:4096].rearrange("p (c n) -> p c n", c=4), [("hT", q) for q in range(3)], "xsA"),
              1: (U_f[:, :, :].rearrange("p k t -> p (k t)")[:, 0:4096].rearrange("p (c n) -> p c n", c=4), [("U", q) for q in range(3)], "xsB")}
    P.wait_all_keys.add("xsA")
    P.wait_all_keys.add("xsB")

    def stage_loads(si):
        view3, res, key = xstage[si]
        for c in range(4):
            r0 = 1024 + si * 512 + c * 128
            P.add("sp", (lambda e, c=c, r0=r0: e.dma_start(out=view3[:, c, :], in_=x_p[r0:r0 + 128, :])),
                  writes=res, dma_key=key, group=key)

    def staged_to_xT(si):
        view3, res, key = xstage[si]
        for c in range(4):
            col0 = si * 512 + c * 128
            for half in range(2):
                b = nb()
                for j in range(4):
                    k = half * 4 + j
                    transp(ps[b][:, j * 128:(j + 1) * 128], view3[:, c, k * 128:(k + 1) * 128], identf[:], res, b)
                dstv = xT[:, half * 4:half * 4 + 4, col0:col0 + 128]
                srcv = ps[b][:, :].rearrange("p (j t) -> p j t", j=4)
                if half == 0:
                    P.add("act", (lambda e, dstv=dstv, srcv=srcv: e.activation(out=dstv, in_=srcv, func=AF.Copy)),
                          reads=[BK(b)], writes=[XR(si, half * 4 + j) for j in range(4)])
                else:
                    P.add("dve", (lambda e, dstv=dstv, srcv=srcv: e.tensor_copy(out=dstv, in_=srcv)),
                          reads=[BK(b)], writes=[XR(si, half * 4 + j) for j in range(4)])

    SUB_P = [(0, 0, 512), (1, 512, 512)]
    SUB_S = [(2, 1024, 128)]

    early_x = all(k not in skip for k in ("kv", "load"))
    prefetch(3)
    if "states" not in skip:
        for half in range(2):
            i = io_ctr[0] % NIO
            io_ctr[0] += 1
            P.add("sp", (lambda e, i=i, half=half: e.dma_start(out=io[i][0:120, 0:512], in_=st_pool[half * 120:(half + 1) * 120, :])),
                  writes=[("io", i)], dma_key=f"io{i}")
            b = nb()
            for g in range(4):
                transp(ps[b][:, g * 120:(g + 1) * 120], io[i][0:120, g * 128:(g + 1) * 128], identf[0:120, 0:120], [("io", i)], b)
            P.add("act", (lambda e, b=b, half=half: e.activation(
                out=poolhT[:, :, half * 120:(half + 1) * 120], in_=ps[b][:, 0:480].rearrange("p (g r) -> p g r", g=4), func=AF.Copy)),
                reads=[BK(b)], writes=["poolhT"])
        i = io_ctr[0] % NIO
        io_ctr[0] += 1
        P.add("sp", (lambda e, i=i: e.dma_start(out=io[i][0:32, 0:512], in_=st_conv[:, :])), writes=[("io", i)], dma_key=f"io{i}")
        b = nb()
        for g in range(4):
            transp(ps[b][:, g * 32:(g + 1) * 32], io[i][0:32, g * 128:(g + 1) * 128], identf[0:32, 0:32], [("io", i)], b)
        P.add("act", (lambda e, b=b: e.activation(out=convhT[:, :, :], in_=ps[b][:, 0:128].rearrange("p (g r) -> p g r", g=4), func=AF.Copy)),
              reads=[BK(b)], writes=["convhT"])

    if "kv" not in skip:
        memT = scr[:, 0:2048].rearrange("p (k t) -> p k t", k=8)
        for mc in range(2):
            load_tokens(mem[mc * 128:(mc + 1) * 128, :], mc * 128, 0, dst3=memT, res_of=(lambda k: "R0"))
        if early_x:
            for tcn in range(8):
                load_tokens(x_p[tcn * 128:(tcn + 1) * 128, :], tcn * 128, tcn // 4)
            load_tokens(x_s[:, :], 1024, 2)
        mnT = scr_bf[:, 5632:5632 + 2048].rearrange("p (k t) -> p k t", k=8)
        if "kv_norm" not in skip:
            P.add("act", (lambda e: e.activation(out=sq[:, :, 0:256], in_=memT[:, :, :], func=AF.Square)), reads=["R0"], writes=SQ_ALL)
            b = nb()
            mm_group(ps[b][:, 0:256], [(ones_n[:], sq[:, k, 0:256]) for k in range(8)], SQ_ALL + ["c_ones_n"], b)
            r = rstd[0]
            P.add("act", (lambda e, b=b, r=r: e.activation(out=r[:, 0:256], in_=ps[b][:, 0:256], func=AF.Ln, bias=EPS, scale=1.0)),
                  reads=[BK(b)], writes=[("rstd", 0)])
            P.add("act", (lambda e, r=r: e.activation(out=r[:, 0:256], in_=r[:, 0:256], func=AF.Exp, scale=-0.5)), reads=[("rstd", 0)], writes=[("rstd", 0)])
            for k in range(8):
                P.add("dve", (lambda e, k=k, r=r: e.scalar_tensor_tensor(
                    out=mnT[:, k, :], in0=memT[:, k, :], scalar=gains[:, 3, k:k + 1], in1=r[:, 0:256], op0=ALU.mult, op1=ALU.mult)),
                    reads=["R0", ("rstd", 0), "c_gains"], writes=["R2"])
    stg_ctr = [0]

    def kv_tok(wb, dst_d, is_v):
        for mc in range(2):
            for cb in range(2):
                blk = wb[cb]
                sv = slot_view_blk(blk, 8, 512)
                b = nb()
                mm_group(ps[b][:, :], [(mnT[:, k, mc * 128:(mc + 1) * 128], sv[:, k, :]) for k in range(8)],
                         wres(blk, 1) + ["R2"], b)
                i = stg_ctr[0] % 2
                stg_ctr[0] += 1
                P.add("act", (lambda e, b=b, i=i: e.activation(out=stg[i][:, :], in_=ps[b][:, :], func=AF.Copy)),
                      reads=[BK(b)], writes=[("stg", i)])
                if is_v:
                    P.add("dve", (lambda e, i=i, mc=mc, cb=cb: e.tensor_copy(out=Vp[:, mc, cb * 512:(cb + 1) * 512], in_=stg[i][:, :])),
                          reads=[("stg", i)], writes=["Vp"])
                P.add("sp", (lambda e, i=i, mc=mc, cb=cb: e.dma_start(
                    out=dst_d[mc * 128:(mc + 1) * 128, cb * 512:(cb + 1) * 512], in_=stg[i][:, :])),
                    reads=[("stg", i)], dma_key=f"stg{i}")

    def kv_hook_k():
        xk = sched["xk"]
        prefetch(xk[1] + 2)
        for oc in range(8):
            blk = xk[oc // 4]
            sv = slot_view_blk(blk, 8, 512)
            b = nb()
            mm_group(ps[b][:, 0:256], [(sv[:, k, (oc % 4) * 128:(oc % 4 + 1) * 128], mnT[:, k, :]) for k in range(8)],
                     wres(blk, 1) + ["R2"], b)
            P.add("act", (lambda e, b=b, oc=oc: e.activation(out=kTp[:, oc, :], in_=ps[b][:, 0:256], func=AF.Copy)),
                  reads=[BK(b)], writes=["kTp"])
        kv_tok(xk, nk_p, False)

    def kv_hook_v():
        xv = sched["xv"]
        prefetch(xv[1] + 2)
        kv_tok(xv, nv_p, True)

    def dbg_dump(stage):
        if dbg_stage == stage:
            P.add("sp", (lambda e: e.dma_start(out=dbg[:, :], in_=xT[:, :, :].rearrange("p k t -> p (k t)"))),
                  reads=[XR(si, k) for si in range(3) for k in range(8)], dma_key="dbg")

    ETs2 = [ETs, ETsB]
    kT3 = kTs.rearrange("p (c m) -> p c m", c=8)

    def sa_load_k(i):
        P.add("pool", (lambda e: e.dma_start(out=Knat.rearrange("p (mc n) -> p mc n", mc=2),
                                             in_=ck[i * 256:(i + 1) * 256, :].rearrange("(mc p) n -> p mc n", p=128))),
              writes=["R0"], dma_key="kn")

    def sa_load_v(i):
        P.add("pool", (lambda e: e.dma_start(out=Vnat.rearrange("p (mc n) -> p mc n", mc=2),
                                             in_=cv[i * 256:(i + 1) * 256, :].rearrange("(mc p) n -> p mc n", p=128))),
              writes=["R1"], dma_key="vn")

    def sa_s1(i):
        for mc in range(2):
            for quad in range(2):
                b = nb()
                for j in range(4):
                    c = quad * 4 + j
                    transp(ps_bf[b][:, j * 128:(j + 1) * 128], Knat[:, mc * 1024 + c * 128: mc * 1024 + (c + 1) * 128], identb[:], ["R0"], b)
                dstv = kT3[:, quad * 4:quad * 4 + 4, mc * 128:(mc + 1) * 128]
                srcv = ps_bf[b][:, 0:512].rearrange("p (j m) -> p j m", j=4)
                if quad == 0:
                    P.add("act", (lambda e, dstv=dstv, srcv=srcv: e.activation(out=dstv, in_=srcv, func=AF.Copy)), reads=[BK(b)], writes=["R2"])
                else:
                    P.add("dve", (lambda e, dstv=dstv, srcv=srcv: e.tensor_copy(out=dstv, in_=srcv)), reads=[BK(b)], writes=["R2"])

    def sa_s2(i):
        b = nb()
        Eb = ETs2[i % 2]
        for h in range(4):
            for mc in range(2):
                col = (h * 2 + mc) * 8
                mm_group(ps[b][:, col:col + 8],
                         [(kT3[:, h * 2 + dc, mc * 128:(mc + 1) * 128], qTs[:, h * 2 + dc, i * 8:(i + 1) * 8]) for dc in range(2)],
                         ["R2", "qTs"], b)
        P.add("act", (lambda e: e.activation(out=Eb[:, :], in_=ps[b][:, 0:64], func=AF.Exp, scale=1.0 / 16.0)),
              reads=[BK(b)], writes=[("ETs", i % 2)])

    def sa_s3(i):
        Eb = ETs2[i % 2]
        er = ("ETs", i % 2)
        bd = nb()
        E4 = Eb[:, :].rearrange("p (h mc t) -> p h mc t", h=4, mc=2)
        mm_group(ps[bd][:, 0:32].rearrange("p (h t) -> p h t", h=4), [(ones1[:], E4[:, :, mc, :]) for mc in range(2)], [er, "c_ones1"], bd)
        P.add("dve", (lambda e: e.reciprocal(out=rdens[:, :], in_=ps[bd][:, 0:32])), reads=[BK(bd)], writes=["rdens"])
        bo = nb()
        for h in range(4):
            for dc in range(2):
                c = h * 2 + dc
                mm_group(ps[bo][:, c * 8:(c + 1) * 8],
                         [(Vnat[:, mc * 1024 + c * 128: mc * 1024 + (c + 1) * 128], E4[:, h, mc, :]) for mc in range(2)],
                         ["R1", er], bo)
        for dc in range(2):
            o4 = oTs[:, :, i * 8:(i + 1) * 8].rearrange("p (h dc) t -> p h dc t", dc=2)[:, :, dc, :]
            p4 = ps[bo][:, 0:64].rearrange("p (h dc t) -> p h dc t", h=4, dc=2)[:, :, dc, :]
            r4 = rdens[:, :].rearrange("p (h t) -> p h t", h=4)
            P.add("dve", (lambda e, o4=o4, p4=p4, r4=r4: e.tensor_tensor(out=o4, in0=p4, in1=r4, op=ALU.mult)),
                  reads=[BK(bo), "rdens"], writes=["oTs"])

    def sa_hook(t):
        if 0 <= t - 2 < 16:
            sa_s3(t - 2)
        if 0 <= t - 1 < 16:
            sa_load_v(t - 1)
            sa_s2(t - 1)
        if t < 16:
            sa_s1(t)
            if t + 1 < 16:
                sa_load_k(t + 1)

    yT3 = yT.rearrange("p (k t) -> p k t", k=8)
    cur_pass = [0]

    yT3b = scr2[:, 0:4096].rearrange("p (k t) -> p k t", k=8)

    def final_norm(sub, which=0, bss=None):
        ybuf = yT3 if which == 0 else yT3b
        yres = ["R0", "R1", "R2"] if which == 0 else ["Q0", "Q1", "Q2"]
        r, rr, xr = norm_sub(xT, sub, 5, None, bss=bss)
        norm_apply(xT, sub, 5, (lambda k, c0_, n_: ybuf[:, k, 0:n_]), r, rr, xr, (lambda si_: yres))

    def final_out_chunk(sub, tcn, pss_, which=0):
        (si, c0, n) = sub
        ybuf = yT3 if which == 0 else yT3b
        yres = ["R0", "R1", "R2"] if which == 0 else ["Q0", "Q1", "Q2"]
        i = io_ctr[0] % NIO
        io_ctr[0] += 1
        for half in range(2):
            b = nb()
            for j in range(4):
                k = half * 4 + j
                transp(ps[b][:, j * 128:(j + 1) * 128], ybuf[:, k, tcn * 128:(tcn + 1) * 128], identf[:], yres, b)
            if half == 0:
                P.add("act", (lambda e, b=b, i=i: e.activation(out=io[i][:, 0:512], in_=ps[b][:, :], func=AF.Copy)), reads=[BK(b)], writes=[("io", i)])
            else:
                P.add("dve", (lambda e, b=b, i=i: e.tensor_copy(out=io[i][:, 512:1024], in_=ps[b][:, :])), reads=[BK(b)], writes=[("io", i)])
        if si < 2:
            r0 = pss_ * 1024 + c0 + tcn * 128
            dst = y_p[r0:r0 + 128, :]
        else:
            dst = y_s[:, :]
        P.add("sp", (lambda e, i=i, dst=dst: e.dma_start(out=dst, in_=io[i][:, :])), reads=[("io", i)], dma_key=f"io{i}")

    def final_sub(sub, bss=None):
        final_norm(sub, 0, bss=bss)
        for tcn in range(sub[2] // 128):
            final_out_chunk(sub, tcn, cur_pass[0], 0)

    def next_norm(gi):
        def f(sub, bss=None):
            r, rr, xr = norm_sub(xT, sub, gi, hT, bss=bss)
            norm_apply(xT, sub, gi, (lambda k, c0, n: hT[:, k, c0:c0 + n]), r, rr, xr, (lambda si: [("hT", si)]))
        return f

    preload_next = all(k not in skip for k in ("final", "ffn2", "load", "ffn1"))
    deferred_out = []
    for pss in range(2):
        cur_pass[0] = pss
        subs_a = (SUB_S if pss == 0 else []) + SUB_P
        subs_b = (SUB_S if pss == 1 else []) + SUB_P
        if "load" not in skip and not (pss == 1 and preload_next) and not (pss == 0 and early_x):
            for tcn in range(8):
                r0 = pss * 1024 + tcn * 128
                load_tokens(x_p[r0:r0 + 128, :], tcn * 128, tcn // 4)
            if pss == 0:
                load_tokens(x_s[:, :], 1024, 2)
        dbg_dump(("load", pss))

        if "ffn1" not in skip:
            ffn(("f1", pss), 0, subs_a, prenormed=(pss == 1 and preload_next), after=(next_norm(1) if "mix" not in skip else None),
                hooks=(deferred_out if (pss == 1 and deferred_out) else None),
                blk_hooks=({1: kv_hook_k, 2: kv_hook_v} if (pss == 0 and "kv" not in skip) else None))
        dbg_dump(("ffn1", pss))

        if "mix" not in skip:
            if "ffn1" in skip:
                norm(xT, subs_a, 1, hT)
            ablk, heads = sched[("mi", pss)]
            prefetch(ablk + 3)
            sva = slot_view_blk(ablk, 8, 512)
            has_s = (pss == 0)
            DTf_all = sq[:, :, :].rearrange("p k t -> p (k t)")

            def mkset(q):
                base = scr if q == 0 else scr2
                nm = ("R0", "R1", "R2") if q == 0 else ("Q0", "Q1", "Q2")
                A0_, S1_, S2_ = base[:, 0:1408], base[:, 1408:2816], base[:, 2816:4224]
                sv3 = lambda t_: t_[:, 1040:1040 + 368].rearrange("p (s r) -> p s r", s=16)
                return dict(A0=A0_, S1=S1_, S2=S2_, A0s=sv3(A0_), S1s=sv3(S1_), S2s=sv3(S2_), nm=nm,
                            DT=DTf_all[:, q * 2048:q * 2048 + 1152], dtr=("sq", q * 4), dtr_all=[("sq", q * 4 + z_) for z_ in range(3)],
                            UGs=A0_[:, 1040:1040 + 160].rearrange("p (s r) -> p s r", s=16),
                            Ys=S1_[:, 1024:1152].rearrange("p (s t) -> p s t", s=16))
            sets = [mkset(0), mkset(1)]

            def poolA(g, S):
                A0, A0s, r0 = S["A0"], S["A0s"], S["nm"][0]
                if pss == 0:
                    P.add("dve", (lambda e: e.tensor_copy(out=A0[:, 0:15], in_=zero15[:, 0:15])), reads=["c_zero"], writes=[r0])
                else:
                    P.add("dve", (lambda e: e.tensor_copy(out=A0[:, 0:15], in_=ahist[:, g, 0:15])), reads=["ahist"], writes=[r0])
                if has_s:
                    P.add("act", (lambda e: e.activation(out=A0s[:, :, 0:15], in_=poolhT[:, g, :].rearrange("p (s r) -> p s r", s=16),
                                                         func=AF.Copy)), reads=["poolhT"], writes=[r0])
                for (si, c0, n) in subs_a:
                    b = nb()
                    mm_group(ps[b][:, 0:n], [(sva[:, k, g * 128:(g + 1) * 128], hT[:, k, c0:c0 + n]) for k in range(8)],
                             [("w", ablk % 4, 0), ("hT", si)], b)
                    if si < 2:
                        P.add("act", (lambda e, b=b, c0=c0, n=n: e.activation(out=A0[:, 15 + c0:15 + c0 + n], in_=ps[b][:, 0:n], func=AF.Copy)),
                              reads=[BK(b)], writes=[r0])
                    else:
                        P.add("act", (lambda e, b=b: e.activation(out=A0s[:, :, 15:23], in_=ps[b][:, 0:128].rearrange("p (s t) -> p s t", s=16),
                                                                  func=AF.Copy)), reads=[BK(b)], writes=[r0])
                P.add("act", (lambda e: e.activation(out=ahist[:, g, 0:15], in_=A0[:, 1024:1039], func=AF.Copy)), reads=[r0], writes=["ahist"])
                if has_s:
                    P.add("act", (lambda e: e.activation(out=npsT[:, g, :].rearrange("p (s r) -> p s r", s=16), in_=A0s[:, :, 8:23],
                                                         func=AF.Copy)), reads=[r0], writes=["npsT"])

            def poolB(g, S):
                w = 2 << g
                A0, A0s = S["A0"], S["A0s"]
                bufs = [(S["A0"], S["A0s"], S["nm"][0]), (S["S1"], S["S1s"], S["nm"][1]), (S["S2"], S["S2s"], S["nm"][2])]
                cur = 0
                kstep = 1
                nxt_order = [1, 2, 1, 2]
                for lvl in range(g + 1):
                    nxt = nxt_order[lvl]
                    srcb, srcs, srcr = bufs[cur]
                    dstb, dsts, dstr = bufs[nxt]
                    lo = 2 * kstep - 1
                    P.add("dve", (lambda e, srcb=srcb, dstb=dstb, lo=lo, kstep=kstep: e.tensor_tensor(
                        out=dstb[:, lo:1039], in0=srcb[:, lo:1039], in1=srcb[:, lo - kstep:1039 - kstep], op=ALU.add)),
                        reads=[srcr], writes=[dstr])
                    if has_s:
                        P.add("dve", (lambda e, srcs=srcs, dsts=dsts, lo=lo, kstep=kstep: e.tensor_tensor(
                            out=dsts[:, :, lo:23], in0=srcs[:, :, lo:23], in1=srcs[:, :, lo - kstep:23 - kstep], op=ALU.add)),
                            reads=[srcr], writes=[dstr])
                    cur = nxt
                    kstep *= 2
                sb_, ss_, sr_ = bufs[cur]
                DT, dtr, r0 = S["DT"], S["dtr"], S["nm"][0]
                dtw = S["dtr_all"]
                P.add("dve", (lambda e: e.scalar_tensor_tensor(
                    out=DT[:, 0:1024], in0=sb_[:, 15:1039], scalar=1.0 / w, in1=A0[:, 15:1039], op0=ALU.mult, op1=ALU.subtract)),
                    reads=[sr_, r0], writes=dtw)
                if pss == 0:
                    P.add("dve", (lambda e: e.tensor_tensor(out=tmp16[:, :], in0=sb_[:, 15:31], in1=invc[:, g, :], op=ALU.mult)),
                          reads=[sr_, "c_invc"], writes=["tmp16"])
                    P.add("dve", (lambda e: e.tensor_tensor(out=DT[:, 0:16], in0=tmp16[:, :], in1=A0[:, 15:31], op=ALU.subtract)),
                          reads=["tmp16", r0], writes=dtw)
                if has_s:
                    P.add("dve", (lambda e: e.scalar_tensor_tensor(
                        out=DT[:, 1024:1152].rearrange("p (s t) -> p s t", s=16), in0=ss_[:, :, 15:23], scalar=1.0 / w, in1=A0s[:, :, 15:23],
                        op0=ALU.mult, op1=ALU.subtract)), reads=[sr_, r0], writes=dtw)

            def poolC(g, S):
                DT, dtr = S["DT"], S["dtr"]
                for (si, c0, n) in subs_a:
                    b = nb()
                    mm_group(ps[b][:, 0:n], [(poolw[:, g, :], DT[:, c0:c0 + n])], S["dtr_all"] + ["c_poolw"], b)
                    P.add("act", (lambda e, b=b, c0=c0, n=n: e.activation(out=U[:, g, c0:c0 + n], in_=ps[b][:, 0:n], func=AF.Copy,
                                                                          scale=psc[:, g:g + 1])),
                          reads=[BK(b), "c_psc"], writes=[("U", si)])

            def convA(j, S):
                UG, UGs, r0 = S["A0"], S["UGs"], S["nm"][0]
                b_cb, b_cc, b_ch = heads
                if j == 0:
                    prefetch(b_ch + 1)
                sv_cc = slot_view_blk(b_cc, 8, 512)
                sv_ch = slot_view_blk(b_ch, 8, 512)
                if pss == 0:
                    P.add("dve", (lambda e: e.tensor_copy(out=UG[:, 0:2], in_=zero15[:, 0:2])), reads=["c_zero"], writes=[r0])
                else:
                    P.add("dve", (lambda e: e.tensor_copy(out=UG[:, 0:2], in_=uhist[:, j, :])), reads=["uhist"], writes=[r0])
                if has_s:
                    P.add("act", (lambda e: e.activation(out=UGs[:, :, 0:2], in_=convhT[:, j, :].rearrange("p (s r) -> p s r", s=16),
                                                         func=AF.Copy)), reads=["convhT"], writes=[r0])
                for (si, c0, n) in subs_a:
                    bc = nb(); bh = nb()
                    mm_group(ps[bc][:, 0:n], [(sv_cc[:, k, j * 128:(j + 1) * 128], hT[:, k, c0:c0 + n]) for k in range(8)], [("w", b_cc % 4, 0), ("hT", si)], bc)
                    mm_group(ps[bh][:, 0:n], [(sv_ch[:, k, j * 128:(j + 1) * 128], hT[:, k, c0:c0 + n]) for k in range(8)], [("w", b_ch % 4, 0), ("hT", si)], bh)
                    sgi = sg_ctr[0] % 2; sg_ctr[0] += 1
                    s_ = sg[sgi]
                    P.add("act", (lambda e, bc=bc, n=n, s_=s_: e.activation(out=s_[:, 0:n], in_=ps[bc][:, 0:n], func=AF.Copy)),
                          reads=[BK(bc)], writes=[("sg", sgi)])
                    if si < 2:
                        P.add("dve", (lambda e, bh=bh, n=n, s_=s_, c0=c0: e.tensor_tensor(out=UG[:, 2 + c0:2 + c0 + n], in0=ps[bh][:, 0:n], in1=s_[:, 0:n],
                                                                                          op=ALU.mult)), reads=[BK(bh), ("sg", sgi)], writes=[r0])
                    else:
                        P.add("dve", (lambda e, bh=bh, s_=s_: e.tensor_tensor(out=UGs[:, :, 2:10], in0=ps[bh][:, 0:128].rearrange("p (s t) -> p s t", s=16),
                                                                              in1=s_[:, 0:128].rearrange("p (s t) -> p s t", s=16), op=ALU.mult)),
                              reads=[BK(bh), ("sg", sgi)], writes=[r0])
                P.add("act", (lambda e: e.activation(out=uhist[:, j, :], in_=UG[:, 1024:1026], func=AF.Copy)), reads=[r0], writes=["uhist"])
                if has_s:
                    P.add("act", (lambda e: e.activation(out=ncsT[:, j, :].rearrange("p (s r) -> p s r", s=16), in_=UGs[:, :, 8:10], func=AF.Copy)),
                          reads=[r0], writes=["ncsT"])

            def convB(j, S):
                UG, UGs, Y, Ys, r0, r1 = S["A0"], S["UGs"], S["S1"], S["Ys"], S["nm"][0], S["nm"][1]
                segs = [(Y[:, 0:1024], lambda d: UG[:, 2 - d:1026 - d])]
                if has_s:
                    segs.append((Ys, lambda d: UGs[:, :, 2 - d:10 - d]))
                for (yv, uf) in segs:
                    P.add("dve", (lambda e, yv=yv, uf=uf: e.tensor_scalar(out=yv, in0=uf(2), scalar1=cw[:, j:j + 1], scalar2=None, op0=ALU.mult)),
                          reads=[r0, "c_cw"], writes=[r1])
                    for (kk_, d) in ((1, 1), (2, 0)):
                        P.add("dve", (lambda e, yv=yv, uf=uf, kk_=kk_, d=d: e.scalar_tensor_tensor(
                            out=yv, in0=uf(d), scalar=cw[:, kk_ * 4 + j:kk_ * 4 + j + 1], in1=yv, op0=ALU.mult, op1=ALU.add)),
                            reads=[r0, r1, "c_cw"], writes=[r1])

            def convC(j, S):
                Y, r1 = S["S1"], S["nm"][1]
                b_cb = heads[0]
                sv_cb = slot_view_blk(b_cb, 8, 512)
                for (si, c0, n) in subs_a:
                    b = nb()
                    mm_group(ps[b][:, 0:n], [(sv_cb[:, k, j * 128:(j + 1) * 128], hT[:, k, c0:c0 + n]) for k in range(8)], [("w", b_cb % 4, 0), ("hT", si)], b)
                    P.add("dve", (lambda e, b=b, c0=c0, n=n: e.tensor_tensor(out=U[:, 4 + j, c0:c0 + n], in0=ps[b][:, 0:n], in1=Y[:, c0:c0 + n], op=ALU.mult)),
                          reads=[BK(b), r1], writes=[("U", si)])

            steps = [(poolA, poolB, poolC, g) for g in range(4)] + [(convA, convB, convC, j) for j in range(4)]
            for t_ in range(len(steps) + 1):
                if t_ >= 1:
                    _, fb, _, idx = steps[t_ - 1]
                    fb(idx, sets[(t_ - 1) % 2])
                if t_ < len(steps):
                    fa, _, _, idx = steps[t_]
                    fa(idx, sets[t_ % 2])
                if t_ >= 1:
                    _, _, fc, idx = steps[t_ - 1]
                    fc(idx, sets[(t_ - 1) % 2])

            mo = sched[("mo", pss)]
            prefetch(mo[1] + 2)
            proj_accum(mo, subs_a, (lambda si, c0, n, k: U[:, k, c0:c0 + n]), (lambda si: [("U", si)]), False,
                       after=(next_norm(2) if "attn" not in skip else None), after_early=True)
            if pss == 1:
                b = nb()
                for g in range(4):
                    transp(ps[b][0:15, g * 128:(g + 1) * 128], ahist[:, g, 0:15], identf[:], ["ahist"], b)
                i = stg_ctr[0] % 2
                stg_ctr[0] += 1
                P.add("act", (lambda e, b=b, i=i: e.activation(out=stg[i][0:15, :], in_=ps[b][0:15, :], func=AF.Copy)), reads=[BK(b)], writes=[("stg", i)])
                P.add("sp", (lambda e, i=i: e.dma_start(out=np_p[:, :], in_=stg[i][0:15, :])), reads=[("stg", i)], dma_key=f"stg{i}")
                b = nb()
                for j in range(4):
                    transp(ps[b][0:2, j * 128:(j + 1) * 128], uhist[:, j, :], identf[:], ["uhist"], b)
                i = stg_ctr[0] % 2
                stg_ctr[0] += 1
                P.add("act", (lambda e, b=b, i=i: e.activation(out=stg[i][0:2, :], in_=ps[b][0:2, :], func=AF.Copy)), reads=[BK(b)], writes=[("stg", i)])
                P.add("sp", (lambda e, i=i: e.dma_start(out=nc_p[:, :], in_=stg[i][0:2, :])), reads=[("stg", i)], dma_key=f"stg{i}")
            if has_s:
                for half in range(2):
                    b = nb()
                    for g in range(4):
                        transp(ps[b][0:120, g * 128:(g + 1) * 128], npsT[:, g, half * 120:(half + 1) * 120], identf[:], ["npsT"], b)
                    i = stg_ctr[0] % 2
                    stg_ctr[0] += 1
                    P.add("act", (lambda e, b=b, i=i: e.activation(out=stg[i][0:120, :], in_=ps[b][0:120, :], func=AF.Copy)), reads=[BK(b)], writes=[("stg", i)])
                    P.add("sp", (lambda e, i=i, half=half: e.dma_start(out=np_s[half * 120:(half + 1) * 120, :], in_=stg[i][0:120, :])),
                          reads=[("stg", i)], dma_key=f"stg{i}")
                b = nb()
                for j in range(4):
                    transp(ps[b][0:32, j * 128:(j + 1) * 128], ncsT[:, j, :], identf[:], ["ncsT"], b)
                i = stg_ctr[0] % 2
                stg_ctr[0] += 1
                P.add("act", (lambda e, b=b, i=i: e.activation(out=stg[i][0:32, :], in_=ps[b][0:32, :], func=AF.Copy)), reads=[BK(b)], writes=[("stg", i)])
                P.add("sp", (lambda e, i=i: e.dma_start(out=nc_s[:, :], in_=stg[i][0:32, :])), reads=[("stg", i)], dma_key=f"stg{i}")
        dbg_dump(("mix", pss))

        if pss == 0 and "sattn" not in skip:
            sa_load_k(0)
        if "attn" not in skip:
            if "mix" in skip:
                norm(xT, subs_a, 2, hT)
            xq = sched[("xq", pss)]
            prefetch(xq[1] + 2)
            for (si, c0, n) in subs_a:
                for o in range(8):
                    blk = xq[o // 4]
                    sv = slot_view_blk(blk, 8, 512)
                    b = nb()
                    mm_group(ps[b][:, 0:n], [(sv[:, k, (o % 4) * 128:(o % 4 + 1) * 128], hT[:, k, c0:c0 + n]) for k in range(8)],
                             wres(blk, 1) + [("hT", si)], b)
                    if si < 2:
                        P.add("act", (lambda e, b=b, o=o, c0=c0, n=n: e.activation(out=U[:, o, c0:c0 + n], in_=ps[b][:, 0:n], func=AF.Copy)),
                              reads=[BK(b)], writes=[("U", si)])
                    else:
                        P.add("act", (lambda e, b=b, o=o: e.activation(out=qTs[:, o, :], in_=ps[b][:, 0:128], func=AF.Copy)),
                              reads=[BK(b)], writes=["qTs"])
            xo = sched[("xo", pss)]
            prefetch(xo[1] + 2)
            items = [(si, c0, n, h) for (si, c0, n) in SUB_P for h in range(4)]
            st_ = {}

            def at_s1(i):
                (si, c0, n, h) = items[i]
                bs = [nb(), nb()]
                for mc in range(2):
                    mm_group(ps[bs[mc]][:, 0:n],
                             [(kTp[:, h * 2 + dc, mc * 128:(mc + 1) * 128], U[:, h * 2 + dc, c0:c0 + n]) for dc in range(2)],
                             ["kTp", ("U", si)], bs[mc])
                ei = i % 2
                E = ET[ei]
                for mc in range(2):
                    P.add("act", (lambda e, mc=mc, bb=bs[mc]: e.activation(out=E[:, mc, 0:n], in_=ps[bb][:, 0:n], func=AF.Exp, scale=1.0 / 16.0)),
                          reads=[BK(bs[mc])], writes=[("ET", ei)])

            def at_s2(i):
                (si, c0, n, h) = items[i]
                ei = i % 2
                E = ET[ei]
                bd = nb()
                mm_group(ps[bd][:, 0:n], [(ones1[:], E[:, mc, 0:n]) for mc in range(2)], [("ET", ei), "c_ones1"], bd)
                r = rstd[ei]
                P.add("act", (lambda e: e.activation(out=r[:, 0:n], in_=ps[bd][:, 0:n], func=AF.Ln)), reads=[BK(bd)], writes=[("rstd", ei)])
                P.add("act", (lambda e: e.activation(out=r[:, 0:n], in_=r[:, 0:n], func=AF.Exp, scale=-1.0)), reads=[("rstd", ei)], writes=[("rstd", ei)])
                for dc in range(2):
                    bo = nb()
                    c = h * 2 + dc
                    mm_group(ps[bo][:, 0:n], [(Vp[:, mc, c * 128:(c + 1) * 128], E[:, mc, 0:n]) for mc in range(2)], ["Vp", ("ET", ei)], bo)
                    P.add("dve", (lambda e, bo=bo, c=c: e.tensor_tensor(out=hT[:, c, c0:c0 + n], in0=ps[bo][:, 0:n], in1=r[:, 0:n], op=ALU.mult)),
                          reads=[BK(bo), ("rstd", ei)], writes=[("hT", si)])

            at_s1(0)
            for i in range(len(items)):
                if i + 1 < len(items):
                    at_s1(i + 1)
                at_s2(i)

            def xo_rhs(si, c0, n, k):
                return hT[:, k, c0:c0 + n] if si < 2 else oTs[:, k, :]

            proj_accum(xo, subs_b, xo_rhs, (lambda si: [("hT", si)] if si < 2 else ["oTs"]), False,
                       after=(next_norm(4) if "ffn2" not in skip else None), after_early=True)
        dbg_dump(("attn", pss))

        hooks = None
        if pss == 0 and "sattn" not in skip:
            hooks = [(lambda t=t: sa_hook(t)) for t in range(18)]
        if "ffn2" not in skip:
            aft = final_sub if "final" not in skip else None
            if pss == 0 and preload_next:
                def aft(sub, bss=None):
                    (si_, c0_, n_) = sub
                    if si_ == 0:
                        stage_loads(1)
                    final_norm(sub, si_, bss=bss)
                    if si_ == 1:
                        while deferred_out:
                            deferred_out.pop(0)()
                    staged_to_xT(si_)
                    next_norm(0)(sub)
                    for tcn_ in range(4):
                        deferred_out.append((lambda sub=sub, tcn_=tcn_, w_=si_: final_out_chunk(sub, tcn_, 0, w_)))
            ffn(("f2", pss), 4, subs_b, hooks=hooks, prenormed=("attn" not in skip), after=aft, after_early=(pss == 1),
                pre_last=((lambda: stage_loads(0)) if (pss == 0 and preload_next) else None))
        elif hooks is not None:
            for hk in hooks:
                hk()
        dbg_dump(("ffn2", pss))

    assert skip or wb_emitted[0] == len(wblocks), (wb_emitted[0], len(wblocks))
    P.finalize()

    dkeys = sorted(P.dma_count.keys())
    esem = {e: es.enter_context(nc.semaphore(f"s_{e}")) for e in ENGS}
    dsem = {k: es.enter_context(nc.semaphore(f"d_{k}")) for k in dkeys}
    with nc.Block() as block:
        @block.tensor
        def _(t):
            P.emit("pe", t, esem, dsem)

        @block.scalar
        def _(a):
            P.emit("act", a, esem, dsem)

        @block.vector
        def _(v):
            P.emit("dve", v, esem, dsem)

        @block.gpsimd
        def _(g):
            P.emit("pool", g, esem, dsem)

        @block.sync
        def _(s):
            P.emit("sp", s, esem, dsem, final_wait=True)
    es.close()
    return nc


_CACHE = {}


def _consts():
    ident = np.eye(128, dtype=np.float32)
    invc = np.zeros((128, 4, 16), np.float32)
    for g in range(4):
        w = 2 << g
        for t in range(16):
            invc[:, g, t] = 1.0 / min(t + 1, w)
    return ident, invc.reshape(128, 64)


def make_in_maps(inp):
    f = lambda a: np.ascontiguousarray(np.asarray(a, dtype=np.float32))
    ident, invc = _consts()
    gains = f(np.stack([inp["g_ffn1"][0], inp["g_mix"][0], inp["g_xq"][0], inp["g_mem"][0], inp["g_ffn2"][0], inp["g_final"]]))
    gains = f(gains.reshape(6, 8, 128).transpose(2, 0, 1).reshape(128, 48))
    shared = {
        "w_f1i": f(inp["w_ffn1_in"][0]), "w_f1o": f(inp["w_ffn1_out"][0]),
        "w_f2i": f(inp["w_ffn2_in"][0]), "w_f2o": f(inp["w_ffn2_out"][0]),
        "w_mi": f(inp["w_mix_in"][0]), "w_mo": f(inp["w_mix_out"][0]),
        "w_xq": f(inp["w_xq"][0]), "w_xk": f(inp["w_xk"][0]), "w_xv": f(inp["w_xv"][0]), "w_xo": f(inp["w_xo"][0]),
        "pool_w": f(inp["pool_w"][0]).reshape(512, 128),
        "gains": gains,
        "pool_scale": f(f(inp["pool_scale"][0]).reshape(4, 128).T),
        "conv_w": f(f(inp["conv_w"][0]).reshape(12, 128).T),
        "c_ident": ident, "c_invc": invc,
    }
    maps = []
    for c in range(NCORES):
        s0, s1 = 16 * c, 16 * (c + 1)
        m = dict(shared)
        m["x_p"] = f(inp["x_prompt"][c])
        m["x_s"] = f(inp["x_sample"][s0:s1]).reshape(128, D)
        m["mem"] = f(inp["mem_prompt"][c])
        m["st_pool"] = f(inp["state_pool"][0, s0:s1]).reshape(240, 512)
        m["st_conv"] = f(inp["state_conv"][0, s0:s1]).reshape(32, 512)
        m["ck"] = f(inp["cache_mem_k"][0, s0:s1]).reshape(4096, D)
        m["cv"] = f(inp["cache_mem_v"][0, s0:s1]).reshape(4096, D)
        maps.append(m)
    return maps


def kernel(**inputs):
    if "nc" not in _CACHE:
        _CACHE["nc"] = build_program()
    nc = _CACHE["nc"]
    maps = make_in_maps(inputs)
    res = run_bass_kernel_spmd(nc, maps, core_ids=list(range(NCORES)))
    R = res.results
    y_prompt = np.stack([R[c]["y_p"] for c in range(NCORES)]).astype(np.float32)
    y_sample = np.concatenate([R[c]["y_s"].reshape(16, 8, D) for c in range(NCORES)]).astype(np.float32)
    npp = np.stack([R[c]["np_p"] for c in range(NCORES)])[None].astype(np.float32)
    ncp = np.stack([R[c]["nc_p"] for c in range(NCORES)])[None].astype(np.float32)
    nkp = np.stack([R[c]["nk_p"].reshape(256, 4, 256) for c in range(NCORES)])[None].astype(np.float32)
    nvp = np.stack([R[c]["nv_p"].reshape(256, 4, 256) for c in range(NCORES)])[None].astype(np.float32)
    nps = np.concatenate([R[c]["np_s"].reshape(16, 15, 512) for c in range(NCORES)])[None].astype(np.float32)
    ncs = np.concatenate([R[c]["nc_s"].reshape(16, 2, 512) for c in range(NCORES)])[None].astype(np.float32)
    return (y_prompt, y_sample, npp, ncp, nkp, nvp, nps, ncs)
```
